# Optimizing a Trainium2 kernel written in Bass

```python
import math
import jax, jax.numpy as jnp
from jax import lax
import numpy as np

D_MODEL = 2048
BATCH = 1
SEQ = 8192
DEPTH = 1
DEC_BATCH = 8
DEC_SEQ = 4096
PAST_LEN = 128

N_HEADS = 16
N_KV_HEADS = 4
HEAD_DIM = 64
Q_DIM = N_HEADS * HEAD_DIM
KV_DIM = N_KV_HEADS * HEAD_DIM
WINDOW = 128
BLOCK = 128
ROPE_THETA = 10000.0
CONV_DIM = 1024
CONV_K = 31
D_FF = int(math.ceil(8 * D_MODEL / 3 / 256) * 256)
N_MOD = 6
EPS = 1e-6
IN_SPLITS = (Q_DIM, Q_DIM + KV_DIM, Q_DIM + 2 * KV_DIM,
             Q_DIM + 2 * KV_DIM + 2 * CONV_DIM,
             Q_DIM + 2 * KV_DIM + 2 * CONV_DIM + D_MODEL)
IN_DIM = Q_DIM + 2 * KV_DIM + 2 * CONV_DIM + 2 * D_MODEL

kernel_name = "hybrid_swa_conformer_adaln_encoder"


def rmsnorm(x, g):
    xf = x.astype(jnp.float32)
    y = xf * lax.rsqrt(jnp.mean(xf * xf, axis=-1, keepdims=True) + EPS)
    return (y * g.astype(jnp.float32)).astype(x.dtype)


def layernorm(x, g, b):
    xf = x.astype(jnp.float32)
    mu = jnp.mean(xf, axis=-1, keepdims=True)
    var = jnp.mean(jnp.square(xf - mu), axis=-1, keepdims=True)
    y = (xf - mu) * lax.rsqrt(var + EPS)
    return (y * g.astype(jnp.float32) + b.astype(jnp.float32)).astype(x.dtype)


def rope(x):
    S = x.shape[1]
    inv = 1.0 / (ROPE_THETA ** (jnp.arange(0, HEAD_DIM, 2, dtype=jnp.float32) / HEAD_DIM))
    ang = jnp.arange(S, dtype=jnp.float32)[:, None] * inv[None, :]
    cos = jnp.cos(ang)[None, :, None, :]
    sin = jnp.sin(ang)[None, :, None, :]
    xf = x.astype(jnp.float32)
    x1, x2 = xf[..., : HEAD_DIM // 2], xf[..., HEAD_DIM // 2:]
    return jnp.concatenate([x1 * cos - x2 * sin, x2 * cos + x1 * sin], axis=-1).astype(x.dtype)


def band_blocks(t, nb):
    B = t.shape[0]
    tp = jnp.pad(t, ((0, 0), (BLOCK, BLOCK), (0, 0), (0, 0)))
    tp = tp.reshape(B, nb + 2, BLOCK, N_KV_HEADS, HEAD_DIM)
    return jnp.concatenate([tp[:, :-2], tp[:, 1:-1], tp[:, 2:]], axis=2)


def window_attention(q, k, v, sink):
    B, S = q.shape[0], q.shape[1]
    nb = S // BLOCK
    G = N_HEADS // N_KV_HEADS
    qb = q.reshape(B, nb, BLOCK, N_KV_HEADS, G, HEAD_DIM)
    kb = band_blocks(k, nb)
    vb = band_blocks(v, nb)
    s = jnp.einsum('bnqkgd,bnjkd->bnkgqj', qb, kb).astype(jnp.float32) * (HEAD_DIM ** -0.5)
    qi = jnp.arange(BLOCK)[:, None]
    kj = jnp.arange(3 * BLOCK)[None, :]
    rel = kj - BLOCK - qi
    kpos = jnp.arange(nb)[:, None, None] * BLOCK - BLOCK + kj[None]
    valid = (jnp.abs(rel)[None] <= WINDOW) & (kpos >= 0) & (kpos < S)
    s = jnp.where(valid[None, :, None, None], s, -jnp.inf)
    sk = sink.astype(jnp.float32).reshape(1, 1, N_KV_HEADS, G, 1, 1)
    m = jnp.maximum(jnp.max(s, axis=-1, keepdims=True), sk)
    p = jnp.exp(s - m)
    denom = jnp.sum(p, axis=-1, keepdims=True) + jnp.exp(sk - m)
    p = (p / denom).astype(v.dtype)
    o = jnp.einsum('bnkgqj,bnjkd->bnqkgd', p, vb)
    return o.reshape(B, S, N_HEADS * HEAD_DIM)


def conformer_conv(u, conv_w, conv_b, ln_g, ln_b):
    a, g = jnp.split(u, 2, axis=-1)
    h = a * jax.nn.sigmoid(g)
    h = lax.conv_general_dilated(h, conv_w[:, None, :].astype(h.dtype), window_strides=(1,),
                                 padding=[(CONV_K // 2, CONV_K // 2)],
                                 dimension_numbers=('NWC', 'WIO', 'NWC'),
                                 feature_group_count=CONV_DIM)
    h = h + conv_b
    h = layernorm(h, ln_g, ln_b)
    return jax.nn.silu(h)


def encoder(x, c, w_ada, b_ada, g_mix, w_in, attn_sink, w_attn_o, conv_w, conv_b,
            conv_ln_g, conv_ln_b, w_conv_o, w_out, g_ffn, w_ffn_in, w_ffn_out, g_final):
    B, S, _ = x.shape
    for l in range(DEPTH):
        mod = jax.nn.silu(c) @ w_ada[l] + b_ada[l]
        sh1, sc1, gt1, sh2, sc2, gt2 = [t[:, None, :] for t in jnp.split(mod, N_MOD, axis=-1)]
        h = rmsnorm(x, g_mix[l]) * (1.0 + sc1) + sh1
        proj = h @ w_in[l]
        q, k, v, u, ga, gc = jnp.split(proj, IN_SPLITS, axis=-1)
        q = rope(q.reshape(B, S, N_HEADS, HEAD_DIM))
        k = rope(k.reshape(B, S, N_KV_HEADS, HEAD_DIM))
        v = v.reshape(B, S, N_KV_HEADS, HEAD_DIM)
        attn = window_attention(q, k, v, attn_sink[l]) @ w_attn_o[l]
        conv = conformer_conv(u, conv_w[l], conv_b[l], conv_ln_g[l], conv_ln_b[l]) @ w_conv_o[l]
        mix = jax.nn.sigmoid(ga) * attn + jax.nn.sigmoid(gc) * conv
        x = x + gt1 * (mix @ w_out[l])
        h = rmsnorm(x, g_ffn[l]) * (1.0 + sc2) + sh2
        a, b = jnp.split(h @ w_ffn_in[l], 2, axis=-1)
        x = x + gt2 * ((jax.nn.silu(a) * b) @ w_ffn_out[l])
    return rmsnorm(x, g_final)


def setup_inputs(seed: int = 0) -> dict:
    key = jax.random.key(seed)
    ks = jax.random.split(key, 24)
    f32 = jnp.float32

    def nrm(k, shape, scale):
        return jax.random.normal(k, shape, f32) * scale

    L = DEPTH
    return {
        "x_prompt": nrm(ks[0], (BATCH, SEQ, D_MODEL), 1.0),
        "x_sample": nrm(ks[1], (DEC_BATCH, DEC_SEQ, D_MODEL), 1.0),
        "c_prompt": nrm(ks[2], (BATCH, D_MODEL), 1.0),
        "c_sample": nrm(ks[3], (DEC_BATCH, D_MODEL), 1.0),
        "w_ada": nrm(ks[4], (L, D_MODEL, N_MOD * D_MODEL), 0.5 * D_MODEL ** -0.5),
        "b_ada": nrm(ks[5], (L, N_MOD * D_MODEL), 0.01),
        "g_mix": 1.0 + nrm(ks[6], (L, D_MODEL), 0.01),
        "w_in": nrm(ks[7], (L, D_MODEL, IN_DIM), D_MODEL ** -0.5),
        "attn_sink": nrm(ks[8], (L, N_HEADS), 0.5),
        "w_attn_o": nrm(ks[9], (L, Q_DIM, D_MODEL), Q_DIM ** -0.5),
        "conv_w": nrm(ks[10], (L, CONV_K, CONV_DIM), CONV_K ** -0.5),
        "conv_b": nrm(ks[11], (L, CONV_DIM), 0.01),
        "conv_ln_g": 1.0 + nrm(ks[12], (L, CONV_DIM), 0.01),
        "conv_ln_b": nrm(ks[13], (L, CONV_DIM), 0.01),
        "w_conv_o": nrm(ks[14], (L, CONV_DIM, D_MODEL), CONV_DIM ** -0.5),
        "w_out": nrm(ks[15], (L, D_MODEL, D_MODEL), D_MODEL ** -0.5),
        "g_ffn": 1.0 + nrm(ks[16], (L, D_MODEL), 0.01),
        "w_ffn_in": nrm(ks[17], (L, D_MODEL, 2 * D_FF), D_MODEL ** -0.5),
        "w_ffn_out": nrm(ks[18], (L, D_FF, D_MODEL), D_FF ** -0.5),
        "g_final": 1.0 + nrm(ks[19], (D_MODEL,), 0.01),
    }


def reference(x_prompt, x_sample, c_prompt, c_sample, w_ada, b_ada, g_mix, w_in, attn_sink,
              w_attn_o, conv_w, conv_b, conv_ln_g, conv_ln_b, w_conv_o, w_out, g_ffn,
              w_ffn_in, w_ffn_out, g_final):
    y_prompt = encoder(x_prompt, c_prompt, w_ada, b_ada, g_mix, w_in, attn_sink, w_attn_o,
                       conv_w, conv_b, conv_ln_g, conv_ln_b, w_conv_o, w_out, g_ffn,
                       w_ffn_in, w_ffn_out, g_final)
    y_sample = encoder(x_sample, c_sample, w_ada, b_ada, g_mix, w_in, attn_sink, w_attn_o,
                       conv_w, conv_b, conv_ln_g, conv_ln_b, w_conv_o, w_out, g_ffn,
                       w_ffn_in, w_ffn_out, g_final)
    return (y_prompt, y_sample)
```

```python
import numpy as np
import concourse.bass as bass
import concourse.mybir as mybir
from concourse.bass_utils import run_bass_kernel_spmd

F32 = mybir.dt.float32
BF16 = mybir.dt.bfloat16
AF = mybir.ActivationFunctionType
ALU = mybir.AluOpType
AX = mybir.AxisListType

D = 2048
NCH = 16
T = 512
E = 768
QD, KVD, CD, DFF = 1024, 256, 1024, 5632
NFF = DFF // 128
CK = 31
EPS = 1e-6
SLOT = 8192
NSLOT = 3
N_CORES = 8


class Prog:
    ENGS = ("pe", "act", "dve", "pool", "sp")

    def __init__(self):
        self.ops = {e: [] for e in self.ENGS}
        self.cnt = {}
        self.waited = {e: {} for e in self.ENGS}
        self.lw = {}
        self.rd = {}

    def _deps(self, eng, reads, writes):
        need = {}

        def add(k, v):
            if need.get(k, 0) < v:
                need[k] = v
        for r in reads:
            t = self.lw.get(r)
            if t is not None:
                add(*t)
        for w in writes:
            t = self.lw.get(w)
            if t is not None:
                add(*t)
            for k, v in self.rd.get(w, {}).items():
                add(k, v)
        waits = []
        wd = self.waited[eng]
        for k, v in need.items():
            if k == eng == "pe":
                continue
            if wd.get(k, 0) < v:
                wd[k] = v
                waits.append((k, v))
        return waits

    def _record(self, tok, reads, writes):
        k, v = tok
        for r in reads:
            d = self.rd.setdefault(r, {})
            if d.get(k, 0) < v:
                d[k] = v
        for w in writes:
            self.lw[w] = tok
            self.rd[w] = {}

    def op(self, eng, fn, reads=(), writes=()):
        waits = self._deps(eng, reads, writes)
        self.cnt[eng] = self.cnt.get(eng, 0) + 1
        self.ops[eng].append((waits, fn, eng, 1))
        self._record((eng, self.cnt[eng]), reads, writes)

    def dma(self, eng, sem, fn, reads=(), writes=()):
        waits = self._deps(eng, reads, writes)
        self.cnt[sem] = self.cnt.get(sem, 0) + 16
        self.ops[eng].append((waits, fn, sem, 16))
        self._record((sem, self.cnt[sem]), reads, writes)


class V:
    __slots__ = ("ap", "res")

    def __init__(self, ap, res):
        self.ap = ap
        self.res = res


class Region:
    def __init__(self, name, tensor, nbytes):
        self.name = name
        self.t = tensor
        self.nbytes = nbytes

    def view(self, off, n, dt, lo=None, hi=None):
        sz = 4 if dt == F32 else 2
        assert off % sz == 0 and off + n * sz <= self.nbytes, (self.name, off, n, sz, self.nbytes)
        ap = self.t[:, off // 2: off // 2 + n * sz // 2]
        if dt == F32:
            ap = ap.bitcast(F32)
        b0, b1 = off // 1024, (off + n * sz - 1) // 1024
        return V(ap, [(self.name, b) for b in range(b0, b1 + 1)])


def _slab(W, cols, rows=None):
    if rows is not None:
        W = W[rows]
    Wc = W[:, cols]
    K = Wc.shape[0]
    return np.ascontiguousarray(Wc.reshape(K // 128, 128, len(cols)).transpose(1, 0, 2))


def _pp(v):
    return np.ascontiguousarray(v.reshape(-1, 128).T)


def prep_shared(inp):
    f = np.float32
    w_in = np.asarray(inp["w_in"][0], f)
    ar = np.arange
    out = {}
    slabs = []
    for half in range(2):
        cols = []
        for j in range(4):
            for g in range(4):
                cols += list((4 * g + j) * 64 + half * 32 + ar(32))
        slabs.append(_slab(w_in, np.array(cols)))
    cols = []
    for half in range(2):
        for g in range(4):
            cols += list(QD + g * 64 + half * 32 + ar(32))
    slabs.append(_slab(w_in, np.array(cols)))
    slabs.append(_slab(w_in, QD + KVD + ar(256)))
    u0 = QD + 2 * KVD
    for s in range(4):
        cols = []
        for c in (2 * s, 2 * s + 1):
            cols += list(u0 + c * 128 + ar(128)) + list(u0 + CD + c * 128 + ar(128))
        slabs.append(_slab(w_in, np.array(cols)))
    ga0 = u0 + 2 * CD
    for i, s in enumerate(slabs):
        out["win%d" % i] = s.reshape(128, -1)
    w_ao = np.asarray(inp["w_attn_o"][0], f)
    rows = []
    for pi in range(2):
        for j in range(4):
            for gl in range(2):
                g = 2 * pi + gl
                rows += list((4 * g + j) * 64 + ar(64))
    rows = np.array(rows)
    w_co = np.asarray(inp["w_conv_o"][0], f)
    for c in range(16):
        cc = c * 128 + ar(128)
        out["wg%d" % c] = np.concatenate([
            _slab(w_ao, cc, rows=rows).reshape(128, -1), _slab(w_co, cc).reshape(128, -1),
            _slab(w_in, ga0 + cc).reshape(128, -1), _slab(w_in, ga0 + D + cc).reshape(128, -1)], axis=1)
    w_o = np.asarray(inp["w_out"][0], f)
    for nb in range(4):
        out["wo%d" % nb] = _slab(w_o, nb * 512 + ar(512)).reshape(128, -1)
    w_fi = np.asarray(inp["w_ffn_in"][0], f)
    for s in range(22):
        cols = []
        for j in (2 * s, 2 * s + 1):
            cols += list(j * 128 + ar(128)) + list(DFF + j * 128 + ar(128))
        out["wfi%d" % s] = _slab(w_fi, np.array(cols)).reshape(128, -1)
    w_fo = np.asarray(inp["w_ffn_out"][0], f)
    for nb in range(4):
        for kg in range(4):
            out["wfo%d" % (nb * 4 + kg)] = _slab(w_fo[kg * 1408:(kg + 1) * 1408], nb * 512 + ar(512)).reshape(128, -1)
    w_ada = np.asarray(inp["w_ada"][0], f)
    for s in range(24):
        out["wada%d" % s] = _slab(w_ada, s * 512 + ar(512)).reshape(128, -1)
    b_ada = np.asarray(inp["b_ada"][0], f)
    out["bada_pp"] = _pp(b_ada)
    out["bgt_bc"] = np.ascontiguousarray(np.broadcast_to(
        np.concatenate([b_ada[2 * D:3 * D], b_ada[5 * D:6 * D]])[None, :], (128, 2 * D)))
    out["gmix_pp"] = _pp(np.asarray(inp["g_mix"][0], f))
    out["gffn_pp"] = _pp(np.asarray(inp["g_ffn"][0], f))
    out["gf_bc"] = np.ascontiguousarray(np.broadcast_to(np.asarray(inp["g_final"], f)[None, :], (128, D)))
    cw = np.asarray(inp["conv_w"][0], f)
    out["convw_pp"] = np.ascontiguousarray(cw.T.reshape(8, 128, CK).transpose(1, 0, 2)).reshape(128, 8 * CK)
    out["convb_pp"] = _pp(np.asarray(inp["conv_b"][0], f))
    out["lng_pp"] = _pp(np.asarray(inp["conv_ln_g"][0], f))
    out["lnb_pp"] = _pp(np.asarray(inp["conv_ln_b"][0], f))
    sink = np.asarray(inp["attn_sink"][0], f)
    sp = np.zeros((128, 8), f)
    for pi in range(2):
        for j in range(4):
            for gl in range(2):
                sp[gl * 64:(gl + 1) * 64, pi * 4 + j] = sink[4 * (2 * pi + gl) + j]
    out["sink_pp"] = sp
    out["ident"] = np.eye(128, dtype=f)
    kk = ar(128)[:, None]
    qq = ar(128)[None, :]
    out["maskL"] = (kk >= qq).astype(f)
    out["maskR"] = (kk <= qq).astype(f)
    gm = np.zeros((128, 4), f)
    for g in range(4):
        gm[g * 32:(g + 1) * 32, g] = 1.0
    out["gmask"] = gm
    return out


def rope_tables(pos):
    inv = (1.0 / (10000.0 ** (np.arange(0, 64, 2, dtype=np.float32) / 64.0))).astype(np.float32)
    ang = pos.astype(np.float32)[None, :] * inv[:, None]
    c = np.cos(ang).astype(np.float32)
    s = np.sin(ang).astype(np.float32)
    return np.tile(c, (4, 1)), np.tile(s, (4, 1))


def prep_core(segs, cs):
    f = np.float32
    xin, flags, rc, rs = [], [], [], []
    for (xs, st, ln) in segs:
        S = xs.shape[0]
        ext = np.zeros((ln + 256, D), f)
        lo, hi = max(0, st - 128), min(S, st + ln + 128)
        ext[lo - (st - 128): hi - (st - 128)] = xs[lo:hi]
        xin.append(ext)
        for g in range(ln // T):
            p0 = st + g * T - 128
            flags += [1.0 if p0 >= 0 else 0.0, 1.0 if p0 + E <= S else 0.0]
            c, s = rope_tables(np.arange(p0, p0 + E))
            rc.append(c)
            rs.append(s)
    ng = len(rc)
    return {
        "xin": np.ascontiguousarray(np.concatenate(xin, 0)),
        "flags": np.ascontiguousarray(np.broadcast_to(np.array(flags, f)[None, :], (128, 2 * ng))),
        "ropec": np.ascontiguousarray(np.stack(rc, 0)),
        "ropes": np.ascontiguousarray(np.stack(rs, 0)),
        "cT": np.ascontiguousarray(np.concatenate([_pp(cs[0]), _pp(cs[1])], 1)),
    }


class _Stop(Exception):
    pass


def build_program(group_seq, group_row0, dbg=False, stop_at=None):
    NG = len(group_seq)
    nc = bass.Bass("TRN2", target_bir_lowering=False)
    P = Prog()

    def din(name, shape):
        return nc.dram_tensor(name, list(shape), F32, kind="ExternalInput").ap()

    n_ext_rows = max(group_row0) + E
    xin = din("xin", [n_ext_rows, D])
    flags_d = din("flags", [128, 2 * NG])
    ropec_d = din("ropec", [NG, 128, E])
    ropes_d = din("ropes", [NG, 128, E])
    cT_d = din("cT", [128, 32])
    wd = {}
    for i in range(8):
        wd["win%d" % i] = din("win%d" % i, [128, 16 * (256 if i in (2, 3) else 512)])
    for c in range(16):
        wd["wg%d" % c] = din("wg%d" % c, [128, 6144])
    for nb in range(4):
        wd["wo%d" % nb] = din("wo%d" % nb, [128, 16 * 512])
    for s in range(22):
        wd["wfi%d" % s] = din("wfi%d" % s, [128, 16 * 512])
    for s in range(16):
        wd["wfo%d" % s] = din("wfo%d" % s, [128, 11 * 512])
    for s in range(24):
        wd["wada%d" % s] = din("wada%d" % s, [128, 16 * 512])
    small = {}
    for nm, w in (("bada_pp", 96), ("bgt_bc", 2 * D), ("gmix_pp", 16), ("gffn_pp", 16), ("gf_bc", D),
                  ("convw_pp", 8 * CK), ("convb_pp", 8), ("lng_pp", 8), ("lnb_pp", 8), ("sink_pp", 8),
                  ("ident", 128), ("maskL", 128), ("maskR", 128), ("gmask", 4)):
        small[nm] = din(nm, [128, w])
    y_d = nc.dram_tensor("y", [NG * T, D], F32, kind="ExternalOutput").ap()
    dbg_d = {}

    from contextlib import ExitStack
    es = ExitStack()
    with es:
        def sb(name, nbytes):
            t = es.enter_context(nc.sbuf_tensor(name, [128, nbytes // 2], BF16))
            return Region(name, t, nbytes)
        XT = sb("XT", 4 * 8192)
        HT = sb("HT", 16 * E * 2)
        R2 = sb("R2", 46080)
        ZR = sb("ZR", 16384)
        TMP = sb("TMP", 4 * 2048)
        PTR = sb("PTR", 2 * 3072 + 1024)
        ROPE = sb("ROPE", 2 * E * 4)
        BC = sb("BC", 2 * 8192)
        SL = [sb("SL%d" % i, SLOT * 2) for i in range(NSLOT)]
        CST = sb("CST", 4096)
        ps_t = es.enter_context(nc.psum_tensor("ps", [128, 4096], F32))

        sem_names = ["pe", "act", "dve", "pool"] + ["sl%d" % i for i in range(NSLOT)] + ["x%d" % i for i in range(6)] \
            + ["ropec", "ropes", "cst", "cstp", "gf", "bg"] + ["dbg%d" % i for i in range(8)] + ["o%d" % i for i in range(4)]
        sems = {n: es.enter_context(nc.semaphore(n)) for n in sem_names}

        def psb(b, n=512, dt=F32, off=0):
            ap = ps_t[:, b * 512: (b + 1) * 512]
            if dt == BF16:
                ap = ap.bitcast(BF16)
            return V(ap[:, off:off + n], [("ps", b)])

        def psmulti(b0, nb):
            return V(ps_t[:, b0 * 512:(b0 + nb) * 512], [("ps", b) for b in range(b0, b0 + nb)])

        coff = {}
        o = 0
        for nm, nby in (("ident", 256), ("ones", 256), ("maskL", 256), ("maskR", 256),
                        ("convw", 8 * CK * 4), ("convb", 32), ("lng", 32), ("lnb", 32), ("lnbh", 32), ("lngh", 32),
                        ("esink", 32), ("gmask", 16), ("flags", 2 * NG * 4), ("cT", 128), ("scT", 64),
                        ("gmix", 64), ("gffn", 64), ("bada", 384),
                        ("g1", 128), ("sh1", 128), ("g2", 128), ("sh2", 128),
                        ("ss", 64), ("rstd", 64), ("ss3", 128), ("stg", 64)):
            coff[nm] = o
            o += (nby + 3) // 4 * 4
        assert o <= 4096, o

        def cst(nm, n, dt=F32, eoff=0):
            sz = 4 if dt == F32 else 2
            return CST.view(coff[nm] + eoff * sz, n, dt)

        def dump(name, v, n, dt):
            if not dbg or st["dry"]:
                return
            if name in dbg_d:
                return
            dbg_d[name] = nc.dram_tensor("dbg_" + name, [128, n], dt, kind="ExternalOutput").ap()
            P.dma("sp", "dbg%d" % (len(dbg_d) - 1), lambda e, o=dbg_d[name], i=v.ap: e.dma_start(out=o, in_=i), reads=v.res)

        def dma_sp(sem, out_v, in_ap, reads=()):
            P.dma("sp", sem, lambda e, o=out_v.ap, i=in_ap: e.dma_start(out=o, in_=i), reads=reads, writes=out_v.res)

        def dma_cast(sem, out_v, in_ap):
            P.dma("pool", sem, lambda e, o=out_v.ap, i=in_ap: e.dma_start(out=o, in_=i), writes=out_v.res)

        def rr(*vs):
            r = []
            for v in vs:
                r += v.res
            return r

        slab_sched = []
        st = {"next_load": 0, "next_use": 0, "dry": True}

        def slab_view(i, n):
            return SL[i % NSLOT].view(0, n, BF16)

        def prefetch(upto):
            while st["next_load"] <= min(upto, len(slab_sched) - 1):
                i = st["next_load"]
                nm = slab_sched[i]
                n = wd[nm].shape[1]
                dma_cast("sl%d" % (i % NSLOT), slab_view(i, n), wd[nm])
                st["next_load"] += 1

        def use_slab(expect):
            if st["dry"]:
                slab_sched.append(expect)
                return slab_view(len(slab_sched) - 1, wd[expect].shape[1])
            i = st["next_use"]
            assert slab_sched[i] == expect, (slab_sched[i], expect)
            prefetch(i + NSLOT - 1)
            st["next_use"] += 1
            n = wd[expect].shape[1]
            return slab_view(i, n)

        def emit_all():
            for nm, src in (("convw", "convw_pp"), ("convb", "convb_pp"), ("lng", "lng_pp"), ("lnb", "lnb_pp"),
                            ("esink", "sink_pp"), ("gmask", "gmask"), ("gmix", "gmix_pp"), ("gffn", "gffn_pp"), ("bada", "bada_pp")):
                w = small[src].shape[1]
                dma_sp("cst", cst(nm, w), small[src])
            dma_sp("cst", cst("flags", 2 * NG), flags_d)
            dma_sp("cst", cst("cT", 32), cT_d)
            for nm in ("ident", "maskL", "maskR"):
                P.dma("pool", "cstp", lambda e, o=cst(nm, 128, BF16).ap, i=small[nm]: e.dma_start(out=o, in_=i),
                      writes=cst(nm, 128, BF16).res)
            for en_ in ("pe", "act", "dve"):
                P.ops[en_].append(([("cst", P.cnt["cst"]), ("cstp", P.cnt["cstp"])], None, None, 0))
                P.waited[en_]["cst"] = P.cnt["cst"]
                P.waited[en_]["cstp"] = P.cnt["cstp"]
            ones_v = cst("ones", 128, BF16)
            P.op("dve", lambda e: e.memset(ones_v.ap, 1.0), writes=ones_v.res)
            cw_v = cst("convw", 8 * CK)
            P.op("dve", lambda e: e.tensor_scalar(cw_v.ap, cw_v.ap, 0.5, None, ALU.mult), reads=cw_v.res, writes=cw_v.res)
            es_v = cst("esink", 8)
            P.op("act", lambda e: e.activation(es_v.ap, es_v.ap, AF.Exp), reads=es_v.res, writes=es_v.res)
            cT_v = cst("cT", 32)
            scT_v = cst("scT", 32, BF16)
            P.op("act", lambda e: e.activation(scT_v.ap, cT_v.ap, AF.Silu), reads=cT_v.res, writes=scT_v.res)

            screp = PTR.view(0, 16 * 128, BF16)

            def build_screp(slot):
                for k in range(16):
                    src = scT_v.ap[:, slot * 16 + k: slot * 16 + k + 1].to_broadcast([128, 128])
                    dst = screp.ap[:, k * 128:(k + 1) * 128]
                    P.op("dve", lambda e, d=dst, s=src: e.tensor_copy(out=d, in_=s), reads=scT_v.res, writes=screp.res)

            modpp = {0: "sh1", 1: "g1", 3: "sh2", 4: "g2"}
            bank_rr = [0]

            def nbank():
                b = bank_rr[0]
                bank_rr[0] = (b + 1) % 8
                return b

            def mod_pp_block(blk, s4):
                sv = use_slab("wada%d" % (blk * 4 + s4))
                w3 = sv.ap.rearrange("p (k n) -> p k n", k=16)
                b = nbank()
                pv = psb(b, 8)

                def mm(e):
                    ins = None
                    for m in range(4):
                        for k in range(16):
                            ins = e.matmul(pv.ap[:, 2 * m:2 * m + 2], w3[:, k, m * 128:(m + 1) * 128],
                                           scT_v.ap.rearrange("p (s k) -> p k s", s=2)[:, k, :],
                                           start=(k == 0), stop=(k == 15))
                    return ins
                P.op("pe", mm, reads=sv.res + scT_v.res, writes=pv.res)
                nm = modpp[blk]
                for m in range(4):
                    ch = s4 * 4 + m
                    for slot in range(2):
                        dst = cst(nm, 1, F32, slot * 16 + ch)
                        bia = cst("bada", 1, F32, blk * 16 + ch)
                        P.op("dve", lambda e, d=dst, s=pv.ap[:, 2 * m + slot:2 * m + slot + 1], bb=bia:
                             e.tensor_scalar(d.ap, s, bb.ap, None, ALU.add), reads=pv.res + bia.res, writes=dst.res)

            def mod_row_block(which, s4, slot):
                blk = 2 if which == 0 else 5
                sv = use_slab("wada%d" % (blk * 4 + s4))
                w3 = sv.ap.rearrange("p (k n) -> p k n", k=16)
                b = nbank()
                pv = psb(b)

                def mm(e):
                    ins = None
                    for k in range(16):
                        ins = e.matmul(pv.ap, screp.ap[:, k * 128:(k + 1) * 128], w3[:, k, :], start=(k == 0), stop=(k == 15))
                    return ins
                P.op("pe", mm, reads=sv.res + screp.res, writes=pv.res)
                dst = BC.view(which * 8192 + s4 * 2048, 512, F32)
                bsrc = TMP.view(0, 512, F32)
                dma_sp("bg", bsrc, small["bgt_bc"][:, which * D + s4 * 512: which * D + (s4 + 1) * 512])
                P.op("dve", lambda e: e.tensor_tensor(dst.ap, pv.ap, bsrc.ap, ALU.add), reads=pv.res + bsrc.res, writes=dst.res)
                if which == 0:
                    P.op("dve", lambda e: e.tensor_scalar(dst.ap, dst.ap, 0.5, None, ALU.mult), reads=dst.res, writes=dst.res)

            def mod_finish(nm, gn):
                for slot in range(2):
                    d = cst(nm, 16, F32, slot * 16)
                    gsrc = cst(gn, 16)
                    P.op("dve", lambda e, d=d, gsrc=gsrc: e.scalar_tensor_tensor(d.ap, d.ap, 1.0, gsrc.ap, ALU.add, ALU.mult),
                         reads=d.res + gsrc.res, writes=d.res)

            def mod_phase_first(slot):
                for blk in (0, 1):
                    for s4 in range(4):
                        mod_pp_block(blk, s4)
                mod_finish("g1", "gmix")

            def mod_rows(which, slot):
                build_screp(slot)
                for s4 in range(4):
                    mod_row_block(which, s4, slot)

            def mod_ffn_pp():
                for blk in (3, 4):
                    for s4 in range(4):
                        mod_pp_block(blk, s4)
                mod_finish("g2", "gffn")

            def xt(e):
                return XT.view(e * 8192, D, F32)

            def xh(i):
                return ZR.view(i * 8192, D, F32)

            def xtile(e6):
                return xh(0) if e6 == 0 else (xh(1) if e6 == 5 else xt(e6 - 1))

            def hT(c, lo=0, hi=E):
                return HT.view(c * E * 2 + lo * 2, hi - lo, BF16)

            def h2T(c, lo=0, hi=T):
                return HT.view(c * T * 2 + lo * 2, hi - lo, BF16)

            def xnb(e6):
                return R2.view(e6 * 4096, D, BF16)

            JUNK_OFF = 6 * 4096
            O_QT, O_KT, O_VA, O_HG, O_OT, O_CT = 0, 8192, 8192 + 3072, 8192 + 3072 + 6144, 29696, 37888

            def qT(half, j, lo=0, hi=T):
                return R2.view(O_QT + (half * 4 + j) * 1024 + lo * 2, hi - lo, BF16)

            def kT(half, lo=0, hi=E):
                return R2.view(O_KT + half * 1536 + lo * 2, hi - lo, BF16)

            def vaug(e6):
                return R2.view(O_VA + e6 * 1024, 512, BF16)

            def KZ(g4, half, lo=0, hi=E):
                return R2.view(O_HG + (g4 * 2 + half) * 1536 + lo * 2, hi - lo, BF16)

            def vden(e6):
                return ROPE.view(e6 * 1024, 512, BF16)

            def hg(c, lo=0, hi=E):
                return R2.view(O_HG + c * 1536 + lo * 2, hi - lo, BF16)

            def oT(ci, lo=0, hi=T):
                return R2.view(O_OT + ci * 1024 + lo * 2, hi - lo, BF16)

            def cTb(c, lo=0, hi=T):
                return R2.view(O_CT + c * 1024 + lo * 2, hi - lo, BF16)

            def actT(j, lo=0, hi=T):
                return R2.view(j * 1024 + lo * 2, hi - lo, BF16)

            def zt(c):
                return ZR.view(c * 2048, 512, F32)

            def mixT(c, lo=0, hi=T):
                return ZR.view(c * 1024 + lo * 2, hi - lo, BF16)

            def tmp(i, n=512, dt=F32):
                return TMP.view(i * 2048, n, dt)

            ident_v = cst("ident", 128, BF16)
            maskL_v = cst("maskL", 128, BF16)
            maskR_v = cst("maskR", 128, BF16)

            def rstd_batch(src, dst, scale):
                P.op("dve", lambda e: e.tensor_scalar(dst.ap, src.ap, scale, EPS, ALU.mult, ALU.add), reads=src.res, writes=dst.res)
                P.op("act", lambda e: e.activation(dst.ap, dst.ap, AF.Sqrt), reads=dst.res, writes=dst.res)
                P.op("dve", lambda e: e.reciprocal(dst.ap, dst.ap), reads=dst.res, writes=dst.res)

            def rmsnorm_tiles(pairs, pre=None):
                n = len(pairs)
                ssv = cst("ss", n)
                rsv = cst("rstd", n)
                if pre is not None:
                    pv_, ptags = pre
                    P.op("dve", lambda e: e.tensor_reduce(ssv.ap, pv_.ap.rearrange("p (n q) -> p n q", q=4), AX.X, ALU.add),
                         reads=pv_.res + ptags, writes=ssv.res)
                else:
                    cols = [("sscol", i) for i in range(n)]
                    P.op("dve", lambda e: e.memset(ssv.ap, 0.0), writes=ssv.res + cols)
                    for i, (xv, nbv) in enumerate(pairs):
                        acc = ssv.ap[:, i:i + 1]
                        if i % 2 == 0:
                            P.op("act", lambda e, xv=xv, nbv=nbv, acc=acc: e.activation(nbv.ap, xv.ap, AF.Square, accum_out=acc),
                                 reads=xv.res, writes=nbv.res + [cols[i]])
                        else:
                            P.op("dve", lambda e, xv=xv, nbv=nbv, acc=acc: e.scalar_tensor_tensor(nbv.ap, xv.ap, 1.0, xv.ap, ALU.mult, ALU.mult, accum_out=acc),
                                 reads=xv.res, writes=nbv.res + [cols[i]])
                    P.op("dve", lambda e: e.tensor_copy(out=ssv.ap, in_=ssv.ap), reads=cols + ssv.res, writes=ssv.res)
                rstd_batch(ssv, rsv, 1.0 / D)
                for i, (xv, nbv) in enumerate(pairs):
                    if i % 2 == 0:
                        P.op("dve", lambda e, i=i, xv=xv, nbv=nbv: e.tensor_scalar(nbv.ap, xv.ap, rsv.ap[:, i:i + 1], None, ALU.mult),
                             reads=xv.res + rsv.res, writes=nbv.res)
                    else:
                        P.op("act", lambda e, i=i, xv=xv, nbv=nbv: e.activation(nbv.ap, xv.ap, AF.Copy, scale=rsv.ap[:, i:i + 1]),
                             reads=xv.res + rsv.res, writes=nbv.res)

            def group(g):
                slot = group_seq[g]
                modg = (g == 0) or (group_seq[g] != group_seq[g - 1])
                r0 = group_row0[g]
                fl = [cst("flags", 1, F32, 2 * g), cst("flags", 1, F32, 2 * g + 1)]
                for e6 in (0, 5, 1, 2, 3, 4):
                    dma_sp("x%d" % e6, xtile(e6), xin[r0 + e6 * 128: r0 + (e6 + 1) * 128, :])
                rc_v = ROPE.view(0, E, F32)
                rs_v = ROPE.view(E * 4, E, F32)
                dma_sp("ropec", rc_v, ropec_d[g])
                dma_sp("ropes", rs_v, ropes_d[g])
                rmsnorm_tiles([(xtile(e6), xnb(e6)) for e6 in (0, 5, 1, 2, 3, 4)])
                for c in range(NCH):
                    b = nbank()
                    pv = psb(b, E, BF16)

                    def tr(e, c=c, pv=pv):
                        ins = None
                        for e6 in range(6):
                            ins = e.transpose(pv.ap[:, e6 * 128:(e6 + 1) * 128], xnb(e6).ap[:, c * 128:(c + 1) * 128], ident_v.ap)
                        return ins
                    P.op("pe", tr, reads=rr(*[xnb(e6) for e6 in range(6)]) + ident_v.res, writes=pv.res)
                    g1 = cst("g1", 1, F32, slot * 16 + c)
                    s1 = cst("sh1", 1, F32, slot * 16 + c)
                    hv = hT(c)
                    if c % 2 == 0:
                        P.op("act", lambda e, hv=hv, pv=pv, g1=g1, s1=s1: e.activation(hv.ap, pv.ap, AF.Identity, bias=s1.ap, scale=g1.ap),
                             reads=pv.res + g1.res + s1.res, writes=hv.res)
                    else:
                        P.op("dve", lambda e, hv=hv, pv=pv, g1=g1, s1=s1: e.tensor_scalar(hv.ap, pv.ap, g1.ap, s1.ap, ALU.mult, ALU.add),
                             reads=pv.res + g1.res + s1.res, writes=hv.res)

                def proj(sv_ap3, mcol, rhs_lo, rhs_hi, pv, extra_reads, fine=False):
                    if fine:
                        for k in range(16):
                            P.op("pe", lambda e, k=k: e.matmul(pv.ap, sv_ap3[:, k, mcol:mcol + 128], hT(k, rhs_lo, rhs_hi).ap, start=(k == 0), stop=(k == 15)),
                                 reads=extra_reads + hT(k, rhs_lo, rhs_hi).res, writes=pv.res)
                        return
                    def mm(e):
                        ins = None
                        for k in range(16):
                            ins = e.matmul(pv.ap, sv_ap3[:, k, mcol:mcol + 128], hT(k, rhs_lo, rhs_hi).ap, start=(k == 0), stop=(k == 15))
                        return ins
                    P.op("pe", mm, reads=extra_reads + rr(*[hT(k, rhs_lo, rhs_hi) for k in range(16)]), writes=pv.res)

                dump("hT", HT.view(0, 16 * E, BF16), 16 * E, BF16)
                if stop_at == 1:
                    raise _Stop()
                for half in range(2):
                    svq = use_slab("win%d" % half)
                    q3 = svq.ap.rearrange("p (k n) -> p k n", k=16)
                    for j in range(4):
                        proj(q3, j * 128, 128, 640, psb(half * 4 + j), svq.res, fine=(half == 0 and j < 2))

                def rope_pair(x1v, x2v, o1v, o2v, lo, hi):
                    n = hi - lo
                    c_ap, s_ap = rc_v.ap[:, lo:hi], rs_v.ap[:, lo:hi]
                    t1, t2 = tmp(0, n), tmp(1, n)
                    t3, t4 = tmp(2, n), tmp(3, n)
                    rd = rc_v.res + rs_v.res
                    P.op("dve", lambda e: e.tensor_tensor(t1.ap, x1v.ap, c_ap, ALU.mult), reads=x1v.res + rd, writes=t1.res)
                    P.op("dve", lambda e: e.tensor_tensor(t2.ap, x2v.ap, s_ap, ALU.mult), reads=x2v.res + rd, writes=t2.res)
                    P.op("dve", lambda e: e.tensor_tensor(o1v.ap, t1.ap, t2.ap, ALU.subtract), reads=t1.res + t2.res, writes=o1v.res)
                    P.op("dve", lambda e: e.tensor_tensor(t3.ap, x2v.ap, c_ap, ALU.mult), reads=x2v.res + rd, writes=t3.res)
                    P.op("dve", lambda e: e.tensor_tensor(t4.ap, x1v.ap, s_ap, ALU.mult), reads=x1v.res + rd, writes=t4.res)
                    P.op("dve", lambda e: e.tensor_tensor(o2v.ap, t3.ap, t4.ap, ALU.add), reads=t3.res + t4.res, writes=o2v.res)

                for j in range(4):
                    rope_pair(psb(j), psb(4 + j), qT(0, j), qT(1, j), 128, 640)
                svk = use_slab("win2")
                k3 = svk.ap.rearrange("p (k n) -> p k n", k=16)
                for half in range(2):
                    for hh in range(2):
                        proj(k3, half * 128, hh * 384, (hh + 1) * 384, psb(half * 2 + hh, 384), svk.res)
                for hh in range(2):
                    rope_pair(psb(hh, 384), psb(2 + hh, 384), kT(0, hh * 384, (hh + 1) * 384), kT(1, hh * 384, (hh + 1) * 384),
                              hh * 384, (hh + 1) * 384)
                svv = use_slab("win3")
                v3 = svv.ap.rearrange("p (k n) -> p k n", k=16)
                for e6 in range(6):
                    pv = psb(4 + (e6 % 4), 256)

                    def mm(e, e6=e6, pv=pv):
                        ins = None
                        for k in range(16):
                            ins = e.matmul(pv.ap, hT(k, e6 * 128, (e6 + 1) * 128).ap, v3[:, k, :], start=(k == 0), stop=(k == 15))
                        return ins
                    P.op("pe", mm, reads=svv.res + rr(*[hT(k, e6 * 128, (e6 + 1) * 128) for k in range(16)]), writes=pv.res)
                    va, vd = vaug(e6), vden(e6)
                    vo5 = va.ap.rearrange("p (a b c) -> p a b c", a=2, b=2)
                    vd5 = vd.ap.rearrange("p (a b c) -> p a b c", a=2, b=2)
                    pv4 = pv.ap.rearrange("p (a b d) -> p a b d", a=2, b=2)
                    P.op("dve", lambda e, va=va: e.memset(va.ap, 0.0), writes=va.res)
                    P.op("dve", lambda e, vd=vd: e.memset(vd.ap, 0.0), writes=vd.res)
                    for gl in range(2):
                        o_ = vo5[:, :, gl, gl * 64:(gl + 1) * 64]
                        d_ = vd5[:, :, gl, gl * 64:(gl + 1) * 64]
                        i_ = pv4[:, :, gl, :]
                        if e6 in (0, 5):
                            f = fl[0] if e6 == 0 else fl[1]
                            P.op("dve", lambda e, o_=o_, i_=i_, f=f: e.tensor_scalar(o_, i_, f.ap, None, ALU.mult),
                                 reads=pv.res + f.res, writes=va.res)
                            P.op("dve", lambda e, d_=d_, f=f: e.tensor_copy(out=d_, in_=f.ap.unsqueeze(1).to_broadcast([128, 2, 64])),
                                 reads=f.res, writes=vd.res)
                        else:
                            P.op("act", lambda e, o_=o_, i_=i_: e.activation(o_, i_, AF.Copy), reads=pv.res, writes=va.res)
                            P.op("dve", lambda e, d_=d_: e.memset(d_, 1.0), writes=vd.res)
                UL, UN = 113, 271
                NPE = CK

                def dve_taps(c):
                    zv = zt(c)
                    cb = cst("convb", 1, F32, c)
                    for j in range(NPE, CK):
                        wj = cst("convw", 1, F32, c * CK + j)
                        hv = hg(c, j, j + T)
                        if j == NPE:
                            P.op("dve", lambda e, zv=zv, hv=hv, wj=wj, cb=cb: e.tensor_scalar(zv.ap, hv.ap, wj.ap, cb.ap, ALU.mult, ALU.add),
                                 reads=hv.res + wj.res + cb.res, writes=zv.res)
                        else:
                            P.op("dve", lambda e, zv=zv, hv=hv, wj=wj: e.scalar_tensor_tensor(zv.ap, hv.ap, wj.ap, zv.ap, ALU.mult, ALU.add),
                                 reads=hv.res + wj.res + zv.res, writes=zv.res)

                for s in range(4):
                    svu = use_slab("win%d" % (4 + s))
                    u3 = svu.ap.rearrange("p (k n) -> p k n", k=16)
                    for cc in range(2):
                        c = 2 * s + cc
                        base = (c % 2) * 4
                        for hh in range(2):
                            lo = UL + hh * UN
                            proj(u3, cc * 256, lo, lo + UN, psb(base + hh, UN), svu.res)
                            proj(u3, cc * 256 + 128, lo, lo + UN, psb(base + 2 + hh, UN), svu.res)
                        for hh in range(2):
                            th = tmp(hh, UN)
                            pa, pg = psb(base + hh, UN), psb(base + 2 + hh, UN)
                            hv = hg(c, hh * UN, (hh + 1) * UN)
                            P.op("act", lambda e, th=th, pg=pg: e.activation(th.ap, pg.ap, AF.Tanh, scale=0.5), reads=pg.res, writes=th.res)
                            P.op("dve", lambda e, th=th, pa=pa, hv=hv: e.scalar_tensor_tensor(hv.ap, th.ap, 1.0, pa.ap, ALU.add, ALU.mult),
                                 reads=th.res + pa.res, writes=hv.res)
                        for side in range(2):
                            hv = hg(c, 0, 15) if side == 0 else hg(c, 15 + T, 30 + T)
                            P.op("dve", lambda e, hv=hv, f=fl[side]: e.tensor_scalar(hv.ap, hv.ap, f.ap, None, ALU.mult),
                                 reads=hv.res + f.res, writes=hv.res)
                        if c > 0 and NPE < CK:
                            dve_taps(c - 1)
                if NPE < CK:
                    dve_taps(7)
                if stop_at == 2:
                    raise _Stop()
                s1p, s2p = psb(0), psb(1)
                pend_stats = []

                def emit_stats(zb, zq, c):
                    P.op("pe", lambda e: e.matmul(s1p.ap, ones_v.ap, zb.ap, start=(c == 0), stop=(c == 7)),
                         reads=zb.res + ones_v.res, writes=s1p.res)
                    P.op("pe", lambda e: e.matmul(s2p.ap, ones_v.ap, zq.ap, start=(c == 0), stop=(c == 7)),
                         reads=zq.res + ones_v.res, writes=s2p.res)

                for c in range(8):
                    pz = psb(2 + (c % 6))
                    for hi_, (j0, j1) in enumerate(((0, NPE // 2), (NPE // 2, NPE))):
                        dg = TMP.view(hi_ * 4096, (j1 - j0) * 128, BF16)
                        nt = j1 - j0
                        wv = cst("convw", nt, F32, c * CK + j0)
                        dg3 = dg.ap.rearrange("p (t m) -> p t m", t=nt)
                        beng = "dve" if hi_ == 0 else "pool"
                        P.op(beng, lambda e, dg3=dg3, wv=wv, nt=nt: e.tensor_tensor(
                            dg3, ident_v.ap.unsqueeze(1).to_broadcast([128, nt, 128]),
                            wv.ap.unsqueeze(2).to_broadcast([128, nt, 128]), ALU.mult),
                             reads=ident_v.res + wv.res, writes=dg.res)

                        def cmm(en, c=c, dg=dg, j0=j0, j1=j1, pz=pz):
                            ins = None
                            for j in range(j0, j1):
                                ins = en.matmul(pz.ap, dg.ap[:, (j - j0) * 128:(j - j0 + 1) * 128], hg(c, j, j + T).ap,
                                                start=(j == 0), stop=(j == NPE - 1))
                            return ins
                        P.op("pe", cmm, reads=dg.res + hg(c, 0, 30 + T).res, writes=pz.res)
                    zv = zt(c)
                    zb = PTR.view((c % 2) * 2048, 512, BF16)
                    zq = PTR.view((c % 2) * 2048 + 1024, 512, BF16)
                    if NPE < CK:
                        P.op("dve", lambda e, zv=zv, pz=pz: e.tensor_tensor(zv.ap, pz.ap, zv.ap, ALU.add), reads=pz.res + zv.res, writes=zv.res)
                    else:
                        cb = cst("convb", 1, F32, c)
                        P.op("act", lambda e, zv=zv, pz=pz, cb=cb: e.activation(zv.ap, pz.ap, AF.Identity, bias=cb.ap), reads=pz.res + cb.res, writes=zv.res)
                    P.op("act", lambda e, zb=zb, zv=zv: e.activation(zb.ap, zv.ap, AF.Copy), reads=zv.res, writes=zb.res)
                    P.op("act", lambda e, zq=zq, zv=zv: e.activation(zq.ap, zv.ap, AF.Square), reads=zv.res, writes=zq.res)
                    pend_stats.append((zb, zq, c))
                    if len(pend_stats) > 1:
                        emit_stats(*pend_stats.pop(0))
                emit_stats(*pend_stats.pop(0))
                for g4 in range(4):
                    gmk = cst("gmask", 1, F32, g4)
                    for half in range(2):
                        kz, ks = KZ(g4, half), kT(half)
                        P.op("dve", lambda e, kz=kz, ks=ks, gmk=gmk: e.tensor_scalar(kz.ap, ks.ap, gmk.ap, None, ALU.mult),
                             reads=ks.res + gmk.res, writes=kz.res)
                mu, rsd, m2 = tmp(0), tmp(1), tmp(2)
                P.op("dve", lambda e: e.tensor_scalar(mu.ap, s1p.ap, 1.0 / CD, None, ALU.mult), reads=s1p.res, writes=mu.res)
                P.op("dve", lambda e: e.tensor_tensor(m2.ap, mu.ap, mu.ap, ALU.mult), reads=mu.res, writes=m2.res)
                P.op("dve", lambda e: e.scalar_tensor_tensor(rsd.ap, s2p.ap, 1.0 / CD, m2.ap, ALU.mult, ALU.subtract),
                     reads=s2p.res + m2.res, writes=rsd.res)
                rstd_batch(rsd, rsd, 1.0)
                def ln_apply(c):
                    zv = zt(c)
                    P.op("dve", lambda e, zv=zv: e.tensor_tensor(zv.ap, zv.ap, mu.ap, ALU.subtract), reads=zv.res + mu.res, writes=zv.res)
                    P.op("dve", lambda e, zv=zv: e.tensor_tensor(zv.ap, zv.ap, rsd.ap, ALU.mult), reads=zv.res + rsd.res, writes=zv.res)
                    lg, lb = cst("lng", 1, F32, c), cst("lnb", 1, F32, c)
                    cv = cTb(c)
                    P.op("act", lambda e, cv=cv, zv=zv, lg=lg, lb=lb: e.activation(cv.ap, zv.ap, AF.Silu, bias=lb.ap, scale=lg.ap),
                         reads=zv.res + lg.res + lb.res, writes=cv.res)
                if stop_at == 3:
                    raise _Stop()
                def att_s(stp):
                    e, j = stp // 4, stp % 4
                    Sv = psmulti((stp % 2) * 3, 3)
                    S4 = Sv.ap.rearrange("p (g k q) -> p g k q", g=4, k=3)

                    def smm(en, e=e, j=j, S4=S4):
                        ins = None
                        for kb in range(3):
                            for g4 in range(4):
                                for half in range(2):
                                    ins = en.matmul(S4[:, g4, kb, :],
                                                    KZ(g4, half, (e + kb) * 128, (e + kb + 1) * 128).ap,
                                                    qT(half, j, e * 128, (e + 1) * 128).ap,
                                                    start=(half == 0), stop=(half == 1))
                        return ins
                    P.op("pe", smm, reads=rr(*[KZ(g4, half, e * 128, (e + 3) * 128) for g4 in range(4) for half in range(2)])
                         + rr(qT(0, j, e * 128, (e + 1) * 128), qT(1, j, e * 128, (e + 1) * 128)), writes=Sv.res)

                def att_p(stp):
                    e, j = stp // 4, stp % 4
                    Sv = psmulti((stp % 2) * 3, 3)
                    Pv = PTR.view((stp % 2) * 3072, 1536, BF16)
                    P4 = Pv.ap.rearrange("p (g k q) -> p g k q", g=4, k=3)
                    for b3 in range(3):
                        P.op("act", lambda en, Pv=Pv, Sv=Sv, b3=b3: en.activation(Pv.ap[:, b3 * 512:(b3 + 1) * 512], Sv.ap[:, b3 * 512:(b3 + 1) * 512], AF.Exp, scale=0.125),
                             reads=Sv.res, writes=Pv.res)
                    P.op("dve", lambda en, P4=P4: en.tensor_tensor(P4[:, :, 0, :], P4[:, :, 0, :],
                                                                    maskL_v.ap.unsqueeze(1).to_broadcast([128, 4, 128]), ALU.mult),
                         reads=Pv.res + maskL_v.res, writes=Pv.res)
                    P.op("pool", lambda en, P4=P4: en.tensor_tensor(P4[:, :, 2, :], P4[:, :, 2, :],
                                                                   maskR_v.ap.unsqueeze(1).to_broadcast([128, 4, 128]), ALU.mult),
                         reads=Pv.res + maskR_v.res, writes=Pv.res)

                def att_o(stp):
                    e, j = stp // 4, stp % 4
                    Pv = PTR.view((stp % 2) * 3072, 1536, BF16)
                    P4 = Pv.ap.rearrange("p (g k q) -> p g k q", g=4, k=3)
                    ODv = psb(6 + (stp % 2))
                    OD3 = ODv.ap.rearrange("p (a q) -> p a q", a=4)

                    def pvm(en, e=e, P4=P4, OD3=OD3):
                        ins = None
                        for pi in range(2):
                            for part in range(2):
                                n_ = 0
                                for gl in range(2):
                                    for kb in range(3):
                                        src = vaug(e + kb) if part == 0 else vden(e + kb)
                                        l3 = src.ap.rearrange("p (g d) -> p g d", g=4)
                                        ins = en.matmul(OD3[:, part * 2 + pi, :], l3[:, 2 * pi + gl, :], P4[:, 2 * pi + gl, kb, :],
                                                        start=(n_ == 0), stop=(n_ == 5))
                                        n_ += 1
                        return ins
                    P.op("pe", pvm, reads=Pv.res + rr(*[vaug(e + kb) for kb in range(3)]) + rr(*[vden(e + kb) for kb in range(3)]),
                         writes=ODv.res)

                def att_n(stp):
                    e, j = stp // 4, stp % 4
                    ODv = psb(6 + (stp % 2))
                    OD3 = ODv.ap.rearrange("p (a q) -> p a q", a=4)
                    Dp = PTR.view(6144, 256, F32)
                    Dp3 = Dp.ap.rearrange("p (a q) -> p a q", a=2)
                    esk = cst("esink", 1, F32, j)
                    for pi in range(2):
                        esk = cst("esink", 1, F32, pi * 4 + j)
                        P.op("dve", lambda en, pi=pi, esk=esk, OD3=OD3, Dp3=Dp3: en.tensor_scalar(Dp3[:, pi, :], OD3[:, 2 + pi, :], esk.ap, None, ALU.add),
                             reads=ODv.res + esk.res, writes=Dp.res)
                    P.op("dve", lambda en, Dp=Dp: en.reciprocal(Dp.ap, Dp.ap), reads=Dp.res, writes=Dp.res)
                    for pi in range(2):
                        ov = oT(pi * 4 + j, e * 128, (e + 1) * 128)
                        P.op("dve", lambda en, pi=pi, ov=ov, OD3=OD3, Dp3=Dp3: en.tensor_tensor(ov.ap, OD3[:, pi, :], Dp3[:, pi, :], ALU.mult),
                             reads=ODv.res + Dp.res, writes=ov.res)

                att_s(0)
                att_p(0)
                for stp in range(16):
                    if stp + 1 < 16:
                        att_s(stp + 1)
                    att_o(stp)
                    if stp + 1 < 16:
                        att_p(stp + 1)
                    att_n(stp)
                for c in range(8):
                    ln_apply(c)
                dump("qT", R2.view(O_QT, 8 * T, BF16), 8 * T, BF16)
                dump("kT", R2.view(O_KT, 2 * E, BF16), 2 * E, BF16)
                dump("va", R2.view(O_VA, 6 * 512, BF16), 6 * 512, BF16)
                dump("oT", R2.view(O_OT, 8 * T, BF16), 8 * T, BF16)
                dump("cT", R2.view(O_CT, 8 * T, BF16), 8 * T, BF16)
                if stop_at == 4:
                    raise _Stop()
                for c in range(NCH):
                    svg = use_slab("wg%d" % c)
                    a3 = svg.ap[:, 0:1024].rearrange("p (k n) -> p k n", k=8)
                    c3 = svg.ap[:, 1024:2048].rearrange("p (k n) -> p k n", k=8)
                    ga3 = svg.ap[:, 2048:4096].rearrange("p (k n) -> p k n", k=16)
                    gc3 = svg.ap[:, 4096:6144].rearrange("p (k n) -> p k n", k=16)
                    cc = 0
                    base = (c % 2) * 4
                    pga, pgc, pao, pco = psb(base), psb(base + 1), psb(base + 2), psb(base + 3)
                    proj(ga3, 0, 128, 640, pga, svg.res)
                    proj(gc3, 0, 128, 640, pgc, svg.res)

                    def aom(en, cc=cc, a3=a3, pao=pao):
                        ins = None
                        for k in range(8):
                            ins = en.matmul(pao.ap, a3[:, k, cc * 128:(cc + 1) * 128], oT(k).ap, start=(k == 0), stop=(k == 7))
                        return ins
                    P.op("pe", aom, reads=svg.res + rr(*[oT(k) for k in range(8)]), writes=pao.res)

                    def com(en, cc=cc, c3=c3, pco=pco):
                        ins = None
                        for k in range(8):
                            ins = en.matmul(pco.ap, c3[:, k, cc * 128:(cc + 1) * 128], cTb(k).ap, start=(k == 0), stop=(k == 7))
                        return ins
                    P.op("pe", com, reads=svg.res + rr(*[cTb(k) for k in range(8)]), writes=pco.res)
                    tha, thc = tmp(0), tmp(1)
                    m1, m2_ = tmp(2), tmp(3)
                    P.op("act", lambda en, tha=tha, pga=pga: en.activation(tha.ap, pga.ap, AF.Tanh, scale=0.5), reads=pga.res, writes=tha.res)
                    P.op("act", lambda en, thc=thc, pgc=pgc: en.activation(thc.ap, pgc.ap, AF.Tanh, scale=0.5), reads=pgc.res, writes=thc.res)
                    P.op("dve", lambda en, m1=m1, tha=tha, pao=pao: en.scalar_tensor_tensor(m1.ap, tha.ap, 1.0, pao.ap, ALU.add, ALU.mult),
                         reads=tha.res + pao.res, writes=m1.res)
                    P.op("dve", lambda en, m2_=m2_, thc=thc, pco=pco: en.scalar_tensor_tensor(m2_.ap, thc.ap, 1.0, pco.ap, ALU.add, ALU.mult),
                         reads=thc.res + pco.res, writes=m2_.res)
                    mv = mixT(c)
                    P.op("dve", lambda en, mv=mv, m1=m1, m2_=m2_: en.tensor_tensor(mv.ap, m1.ap, m2_.ap, ALU.add),
                         reads=m1.res + m2_.res, writes=mv.res)
                if stop_at == 5:
                    raise _Stop()
                if modg:
                    mod_rows(0, slot)
                ss1 = cst("ss3", 16)
                ss1tags = [("ss1col", i) for i in range(16)]
                P.op("dve", lambda en: en.memset(ss1.ap, 0.0), writes=ss1.res + ss1tags)
                for nb in range(4):
                    svo = use_slab("wo%d" % nb)
                    o3 = svo.ap.rearrange("p (k n) -> p k n", k=16)
                    gt = BC.view(nb * 2048, 512, F32)
                    for e in range(4):
                        pv = psb(nbank())

                        def mm(en, e=e, pv=pv, o3=o3):
                            ins = None
                            for k in range(16):
                                ins = en.matmul(pv.ap, mixT(k, e * 128, (e + 1) * 128).ap, o3[:, k, :], start=(k == 0), stop=(k == 15))
                            return ins
                        P.op("pe", mm, reads=svo.res + rr(*[mixT(k, e * 128, (e + 1) * 128) for k in range(16)]), writes=pv.res)
                        tv = tmp((nb * 4 + e) % 4)
                        xv = XT.view(e * 8192 + nb * 2048, 512, F32)
                        P.op("dve", lambda en, tv=tv, pv=pv, gt=gt: en.tensor_tensor(tv.ap, pv.ap, gt.ap, ALU.mult), reads=pv.res + gt.res, writes=tv.res)
                        P.op("dve", lambda en, tv=tv, xv=xv: en.tensor_tensor(xv.ap, xv.ap, tv.ap, ALU.add), reads=tv.res + xv.res, writes=xv.res)
                        junk1 = TMP.view(((nb * 4 + e) % 4) * 2048, 512, BF16)
                        P.op("act", lambda en, xv=xv, e=e, nb=nb, junk1=junk1: en.activation(junk1.ap, xv.ap, AF.Square, accum_out=ss1.ap[:, e * 4 + nb:e * 4 + nb + 1]),
                             reads=xv.res, writes=junk1.res + [ss1tags[e * 4 + nb]])
                dump("mixT", ZR.view(0, 16 * T, BF16), 16 * T, BF16)
                dump("x1", XT.view(0, 4 * D, F32), 4 * D, F32)
                if stop_at == 6:
                    raise _Stop()
                if g == 0:
                    mod_ffn_pp()
                rmsnorm_tiles([(xt(e), xnb(e)) for e in range(4)], pre=(ss1, ss1tags))
                for c in range(NCH):
                    pv = psb(nbank(), T, BF16)

                    def tr(en, c=c, pv=pv):
                        ins = None
                        for e in range(4):
                            ins = en.transpose(pv.ap[:, e * 128:(e + 1) * 128], xnb(e).ap[:, c * 128:(c + 1) * 128], ident_v.ap)
                        return ins
                    P.op("pe", tr, reads=rr(*[xnb(e) for e in range(4)]) + ident_v.res, writes=pv.res)
                    g2 = cst("g2", 1, F32, slot * 16 + c)
                    s2 = cst("sh2", 1, F32, slot * 16 + c)
                    hv = h2T(c)
                    if c % 2 == 0:
                        P.op("act", lambda en, hv=hv, pv=pv, g2=g2, s2=s2: en.activation(hv.ap, pv.ap, AF.Identity, bias=s2.ap, scale=g2.ap),
                             reads=pv.res + g2.res + s2.res, writes=hv.res)
                    else:
                        P.op("dve", lambda en, hv=hv, pv=pv, g2=g2, s2=s2: en.tensor_scalar(hv.ap, pv.ap, g2.ap, s2.ap, ALU.mult, ALU.add),
                             reads=pv.res + g2.res + s2.res, writes=hv.res)
                if stop_at == 7:
                    raise _Stop()
                for s in range(22):
                    svf = use_slab("wfi%d" % s)
                    f3 = svf.ap.rearrange("p (k n) -> p k n", k=16)
                    for jj in range(2):
                        j = 2 * s + jj
                        base = (j % 4) * 2
                        pa, pb = psb(base), psb(base + 1)
                        for (pv, mcol) in ((pa, jj * 256), (pb, jj * 256 + 128)):
                            if j == 0:
                                for k in range(16):
                                    P.op("pe", lambda en, pv=pv, mcol=mcol, f3=f3, k=k: en.matmul(pv.ap, f3[:, k, mcol:mcol + 128], h2T(k).ap, start=(k == 0), stop=(k == 15)),
                                         reads=svf.res + h2T(k).res, writes=pv.res)
                                continue

                            def mm(en, pv=pv, mcol=mcol, f3=f3):
                                ins = None
                                for k in range(16):
                                    ins = en.matmul(pv.ap, f3[:, k, mcol:mcol + 128], h2T(k).ap, start=(k == 0), stop=(k == 15))
                                return ins
                            P.op("pe", mm, reads=svf.res + rr(*[h2T(k) for k in range(16)]), writes=pv.res)
                        sa = tmp(j % 4)
                        av = actT(j)
                        P.op("act", lambda en, sa=sa, pa=pa: en.activation(sa.ap, pa.ap, AF.Silu), reads=pa.res, writes=sa.res)
                        P.op("dve", lambda en, av=av, sa=sa, pb=pb: en.tensor_tensor(av.ap, sa.ap, pb.ap, ALU.mult), reads=sa.res + pb.res, writes=av.res)
                gfv = HT.view(0, D, F32)
                dma_sp("gf", gfv, small["gf_bc"])
                if stop_at == 8:
                    raise _Stop()
                if modg:
                    mod_rows(1, slot)
                ss3 = [cst("stg", 4, F32, e * 4) for e in range(4)]
                ss3all = cst("stg", 16)
                P.op("dve", lambda en: en.memset(ss3all.ap, 0.0), writes=ss3all.res)
                for nb in range(4):
                    accs = [psb((nb % 2) * 4 + e) for e in range(4)]
                    for kg in range(4):
                        svo = use_slab("wfo%d" % (nb * 4 + kg))
                        o3 = svo.ap.rearrange("p (k n) -> p k n", k=11)
                        for e in range(4):
                            def mm(en, e=e, kg=kg, o3=o3, pv=accs[e]):
                                ins = None
                                for k in range(11):
                                    ins = en.matmul(pv.ap, actT(kg * 11 + k, e * 128, (e + 1) * 128).ap, o3[:, k, :],
                                                    start=(kg == 0 and k == 0), stop=(kg == 3 and k == 10))
                                return ins
                            P.op("pe", mm, reads=svo.res + rr(*[actT(kg * 11 + k, e * 128, (e + 1) * 128) for k in range(11)]), writes=accs[e].res)
                    gt = BC.view(8192 + nb * 2048, 512, F32)
                    for e in range(4):
                        pv = accs[e]
                        tv = tmp(e)
                        xv = XT.view(e * 8192 + nb * 2048, 512, F32)
                        junk = R2.view(45056, 512, BF16)
                        P.op("dve", lambda en, tv=tv, pv=pv, gt=gt: en.tensor_tensor(tv.ap, pv.ap, gt.ap, ALU.mult), reads=pv.res + gt.res, writes=tv.res)
                        P.op("dve", lambda en, tv=tv, xv=xv: en.tensor_tensor(xv.ap, xv.ap, tv.ap, ALU.add), reads=tv.res + xv.res, writes=xv.res)
                        P.op("act", lambda en, xv=xv, e=e, nb=nb, junk=junk: en.activation(junk.ap, xv.ap, AF.Square, accum_out=ss3[e].ap[:, nb:nb + 1]),
                             reads=xv.res, writes=junk.res + ss3[e].res)
                ssv4, rsv4 = cst("ss", 4), cst("rstd", 4)
                P.op("dve", lambda en: en.tensor_reduce(ssv4.ap, ss3all.ap.rearrange("p (n q) -> p n q", q=4), AX.X, ALU.add),
                     reads=ss3all.res, writes=ssv4.res)
                rstd_batch(ssv4, rsv4, 1.0 / D)
                for e in range(4):
                    rsv = cst("rstd", 1, F32, e)
                    xv = xt(e)
                    P.op("dve",
                         lambda en, xv=xv, rsv=rsv: en.scalar_tensor_tensor(xv.ap, xv.ap, rsv.ap, gfv.ap, ALU.mult, ALU.mult),
                         reads=xv.res + rsv.res + gfv.res, writes=xv.res)
                    P.dma("sp", "o%d" % e, lambda en, xv=xv, e=e: en.dma_start(out=y_d[g * T + e * 128: g * T + (e + 1) * 128, :], in_=xv.ap),
                          reads=xv.res)

            try:
                mod_phase_first(group_seq[0])
                if stop_at == 0:
                    raise _Stop()
                for g in range(NG):
                    group(g)
                assert st["dry"] or st["next_use"] == len(slab_sched), (st["next_use"], len(slab_sched))
            except _Stop:
                pass

        emit_all()
        P = Prog()
        st.update(next_load=0, next_use=0, dry=False)
        emit_all()

        fin = []
        for i in range(8):
            if P.cnt.get("dbg%d" % i):
                fin.append(("dbg%d" % i, P.cnt["dbg%d" % i]))
        for e in range(4):
            fin.append(("o%d" % e, P.cnt.get("o%d" % e, 0)))

        def replay(engname, handle):
            for (waits, fn, semk, inc) in P.ops[engname]:
                for (k, v) in waits:
                    handle.wait_ge(sems[k], v)
                if fn is None:
                    continue
                ins = fn(handle)
                ins.then_inc(sems[semk], inc)

        with nc.Block() as block:
            @block.tensor
            def _(t):
                replay("pe", t)

            @block.scalar
            def _(a):
                replay("act", a)

            @block.vector
            def _(v):
                replay("dve", v)

            @block.gpsimd
            def _(gp):
                replay("pool", gp)

            @block.sync
            def _(s):
                replay("sp", s)
                for (k, v) in fin:
                    s.wait_ge(sems[k], v)
    return nc


_SHARED_KEYS = None


def kernel(x_prompt, x_sample, c_prompt, c_sample, w_ada, b_ada, g_mix, w_in, attn_sink,
           w_attn_o, conv_w, conv_b, conv_ln_g, conv_ln_b, w_conv_o, w_out, g_ffn,
           w_ffn_in, w_ffn_out, g_final):
    inp = dict(w_ada=w_ada, b_ada=b_ada, g_mix=g_mix, w_in=w_in, attn_sink=attn_sink, w_attn_o=w_attn_o,
               conv_w=conv_w, conv_b=conv_b, conv_ln_g=conv_ln_g, conv_ln_b=conv_ln_b, w_conv_o=w_conv_o,
               w_out=w_out, g_ffn=g_ffn, w_ffn_in=w_ffn_in, w_ffn_out=w_ffn_out, g_final=g_final)
    shared = prep_shared(inp)
    xp = np.asarray(x_prompt, np.float32)
    xs = np.asarray(x_sample, np.float32)
    cp = np.asarray(c_prompt, np.float32)
    csm = np.asarray(c_sample, np.float32)
    n = N_CORES
    S_s = xs.shape[1]
    P_len = xp.shape[1] // n
    group_seq = [0] * (S_s // T) + [1] * (P_len // T)
    group_row0 = [g * T for g in range(S_s // T)] + [S_s + 256 + g * T for g in range(P_len // T)]
    nc = build_program(group_seq, group_row0)
    in_maps = []
    for c in range(n):
        m = dict(shared)
        m.update(prep_core([(xs[c], 0, S_s), (xp[0], c * P_len, P_len)], np.stack([csm[c], cp[0]], 0)))
        in_maps.append(m)
    res = run_bass_kernel_spmd(nc, in_maps, core_ids=list(range(n)))
    y_s = np.stack([res.results[c]["y"][:S_s] for c in range(n)], 0)
    y_p = np.concatenate([res.results[c]["y"][S_s:] for c in range(n)], 0)[None]
    return (y_p.astype(np.float32), y_s.astype(np.float32))
```

```python
import numpy as np
import concourse.bass as bass
import concourse.mybir as mybir
from concourse.bass_utils import run_bass_kernel_spmd

F32 = mybir.dt.float32
BF16 = mybir.dt.bfloat16
AF = mybir.ActivationFunctionType
ALU = mybir.AluOpType
AX = mybir.AxisListType

D = 2048
NCH = 16
T = 512
E = 768
QD, KVD, CD, DFF = 1024, 256, 1024, 5632
NFF = DFF // 128
CK = 31
EPS = 1e-6
SLOT = 8192
NSLOT = 3
N_CORES = 8


class Prog:
    ENGS = ("pe", "act", "dve", "pool", "sp")

    def __init__(self):
        self.ops = {e: [] for e in self.ENGS}
        self.cnt = {}
        self.waited = {e: {} for e in self.ENGS}
        self.lw = {}
        self.rd = {}

    def _deps(self, eng, reads, writes):
        need = {}

        def add(k, v):
            if need.get(k, 0) < v:
                need[k] = v
        for r in reads:
            t = self.lw.get(r)
            if t is not None:
                add(*t)
        for w in writes:
            t = self.lw.get(w)
            if t is not None:
                add(*t)
            for k, v in self.rd.get(w, {}).items():
                add(k, v)
        waits = []
        wd = self.waited[eng]
        for k, v in need.items():
            if k == eng == "pe":
                continue
            if wd.get(k, 0) < v:
                wd[k] = v
                waits.append((k, v))
        return waits

    def _record(self, tok, reads, writes):
        k, v = tok
        for r in reads:
            d = self.rd.setdefault(r, {})
            if d.get(k, 0) < v:
                d[k] = v
        for w in writes:
            self.lw[w] = tok
            self.rd[w] = {}

    def op(self, eng, fn, reads=(), writes=()):
        waits = self._deps(eng, reads, writes)
        self.cnt[eng] = self.cnt.get(eng, 0) + 1
        self.ops[eng].append((waits, fn, eng, 1))
        self._record((eng, self.cnt[eng]), reads, writes)

    def dma(self, eng, sem, fn, reads=(), writes=()):
        waits = self._deps(eng, reads, writes)
        self.cnt[sem] = self.cnt.get(sem, 0) + 16
        self.ops[eng].append((waits, fn, sem, 16))
        self._record((sem, self.cnt[sem]), reads, writes)


class V:
    __slots__ = ("ap", "res")

    def __init__(self, ap, res):
        self.ap = ap
        self.res = res


class Region:
    def __init__(self, name, tensor, nbytes):
        self.name = name
        self.t = tensor
        self.nbytes = nbytes

    def view(self, off, n, dt, lo=None, hi=None):
        sz = 4 if dt == F32 else 2
        assert off % sz == 0 and off + n * sz <= self.nbytes, (self.name, off, n, sz, self.nbytes)
        ap = self.t[:, off // 2: off // 2 + n * sz // 2]
        if dt == F32:
            ap = ap.bitcast(F32)
        b0, b1 = off // 1024, (off + n * sz - 1) // 1024
        return V(ap, [(self.name, b) for b in range(b0, b1 + 1)])


def _slab(W, cols, rows=None):
    if rows is not None:
        W = W[rows]
    Wc = W[:, cols]
    K = Wc.shape[0]
    return np.ascontiguousarray(Wc.reshape(K // 128, 128, len(cols)).transpose(1, 0, 2))


def _pp(v):
    return np.ascontiguousarray(v.reshape(-1, 128).T)


def prep_shared(inp):
    f = np.float32
    w_in = np.asarray(inp["w_in"][0], f)
    ar = np.arange
    out = {}
    slabs = []
    for half in range(2):
        cols = []
        for j in range(4):
            for g in range(4):
                cols += list((4 * g + j) * 64 + half * 32 + ar(32))
        slabs.append(_slab(w_in, np.array(cols)))
    cols = []
    for half in range(2):
        for g in range(4):
            cols += list(QD + g * 64 + half * 32 + ar(32))
    slabs.append(_slab(w_in, np.array(cols)))
    slabs.append(_slab(w_in, QD + KVD + ar(256)))
    u0 = QD + 2 * KVD
    for s in range(4):
        cols = []
        for c in (2 * s, 2 * s + 1):
            cols += list(u0 + c * 128 + ar(128)) + list(u0 + CD + c * 128 + ar(128))
        slabs.append(_slab(w_in, np.array(cols)))
    ga0 = u0 + 2 * CD
    for i, s in enumerate(slabs):
        out["win%d" % i] = s.reshape(128, -1)
    w_ao = np.asarray(inp["w_attn_o"][0], f)
    rows = []
    for pi in range(2):
        for j in range(4):
            for gl in range(2):
                g = 2 * pi + gl
                rows += list((4 * g + j) * 64 + ar(64))
    rows = np.array(rows)
    w_co = np.asarray(inp["w_conv_o"][0], f)
    for c in range(16):
        cc = c * 128 + ar(128)
        out["wg%d" % c] = np.concatenate([
            _slab(w_ao, cc, rows=rows).reshape(128, -1), _slab(w_co, cc).reshape(128, -1),
            _slab(w_in, ga0 + cc).reshape(128, -1), _slab(w_in, ga0 + D + cc).reshape(128, -1)], axis=1)
    w_o = np.asarray(inp["w_out"][0], f)
    for nb in range(4):
        out["wo%d" % nb] = _slab(w_o, nb * 512 + ar(512)).reshape(128, -1)
    w_fi = np.asarray(inp["w_ffn_in"][0], f)
    for s in range(22):
        cols = []
        for j in (2 * s, 2 * s + 1):
            cols += list(j * 128 + ar(128)) + list(DFF + j * 128 + ar(128))
        out["wfi%d" % s] = _slab(w_fi, np.array(cols)).reshape(128, -1)
    w_fo = np.asarray(inp["w_ffn_out"][0], f)
    for nb in range(4):
        for kg in range(4):
            out["wfo%d" % (nb * 4 + kg)] = _slab(w_fo[kg * 1408:(kg + 1) * 1408], nb * 512 + ar(512)).reshape(128, -1)
    w_ada = np.asarray(inp["w_ada"][0], f)
    for s in range(24):
        out["wada%d" % s] = _slab(w_ada, s * 512 + ar(512)).reshape(128, -1)
    b_ada = np.asarray(inp["b_ada"][0], f)
    out["bada_pp"] = _pp(b_ada)
    out["bgt_bc"] = np.ascontiguousarray(np.broadcast_to(
        np.concatenate([b_ada[2 * D:3 * D], b_ada[5 * D:6 * D]])[None, :], (128, 2 * D)))
    out["gmix_pp"] = _pp(np.asarray(inp["g_mix"][0], f))
    out["gffn_pp"] = _pp(np.asarray(inp["g_ffn"][0], f))
    out["gf_bc"] = np.ascontiguousarray(np.broadcast_to(np.asarray(inp["g_final"], f)[None, :], (128, D)))
    cw = np.asarray(inp["conv_w"][0], f)
    out["convw_pp"] = np.ascontiguousarray(cw.T.reshape(8, 128, CK).transpose(1, 0, 2)).reshape(128, 8 * CK)
    out["convb_pp"] = _pp(np.asarray(inp["conv_b"][0], f))
    out["lng_pp"] = _pp(np.asarray(inp["conv_ln_g"][0], f))
    out["lnb_pp"] = _pp(np.asarray(inp["conv_ln_b"][0], f))
    sink = np.asarray(inp["attn_sink"][0], f)
    sp = np.zeros((128, 8), f)
    for pi in range(2):
        for j in range(4):
            for gl in range(2):
                sp[gl * 64:(gl + 1) * 64, pi * 4 + j] = sink[4 * (2 * pi + gl) + j]
    out["sink_pp"] = sp
    out["ident"] = np.eye(128, dtype=f)
    kk = ar(128)[:, None]
    qq = ar(128)[None, :]
    out["maskL"] = (kk >= qq).astype(f)
    out["maskR"] = (kk <= qq).astype(f)
    gm = np.zeros((128, 4), f)
    for g in range(4):
        gm[g * 32:(g + 1) * 32, g] = 1.0
    out["gmask"] = gm
    return out


def rope_tables(pos):
    inv = (1.0 / (10000.0 ** (np.arange(0, 64, 2, dtype=np.float32) / 64.0))).astype(np.float32)
    ang = pos.astype(np.float32)[None, :] * inv[:, None]
    c = np.cos(ang).astype(np.float32)
    s = np.sin(ang).astype(np.float32)
    return np.tile(c, (4, 1)), np.tile(s, (4, 1))


def prep_core(segs, cs):
    f = np.float32
    xin, flags, rc, rs = [], [], [], []
    for (xs, st, ln) in segs:
        S = xs.shape[0]
        ext = np.zeros((ln + 256, D), f)
        lo, hi = max(0, st - 128), min(S, st + ln + 128)
        ext[lo - (st - 128): hi - (st - 128)] = xs[lo:hi]
        xin.append(ext)
        for g in range(ln // T):
            p0 = st + g * T - 128
            flags += [1.0 if p0 >= 0 else 0.0, 1.0 if p0 + E <= S else 0.0]
            c, s = rope_tables(np.arange(p0, p0 + E))
            rc.append(c)
            rs.append(s)
    ng = len(rc)
    return {
        "xin": np.ascontiguousarray(np.concatenate(xin, 0)),
        "flags": np.ascontiguousarray(np.broadcast_to(np.array(flags, f)[None, :], (128, 2 * ng))),
        "ropec": np.ascontiguousarray(np.stack(rc, 0)),
        "ropes": np.ascontiguousarray(np.stack(rs, 0)),
        "cT": np.ascontiguousarray(np.concatenate([_pp(cs[0]), _pp(cs[1])], 1)),
    }


class _Stop(Exception):
    pass


def build_program(group_seq, group_row0, dbg=False, stop_at=None):
    NG = len(group_seq)
    nc = bass.Bass("TRN2", target_bir_lowering=False)
    P = Prog()

    def din(name, shape):
        return nc.dram_tensor(name, list(shape), F32, kind="ExternalInput").ap()

    n_ext_rows = max(group_row0) + E
    xin = din("xin", [n_ext_rows, D])
    flags_d = din("flags", [128, 2 * NG])
    ropec_d = din("ropec", [NG, 128, E])
    ropes_d = din("ropes", [NG, 128, E])
    cT_d = din("cT", [128, 32])
    wd = {}
    for i in range(8):
        wd["win%d" % i] = din("win%d" % i, [128, 16 * (256 if i in (2, 3) else 512)])
    for c in range(16):
        wd["wg%d" % c] = din("wg%d" % c, [128, 6144])
    for nb in range(4):
        wd["wo%d" % nb] = din("wo%d" % nb, [128, 16 * 512])
    for s in range(22):
        wd["wfi%d" % s] = din("wfi%d" % s, [128, 16 * 512])
    for s in range(16):
        wd["wfo%d" % s] = din("wfo%d" % s, [128, 11 * 512])
    for s in range(24):
        wd["wada%d" % s] = din("wada%d" % s, [128, 16 * 512])
    small = {}
    for nm, w in (("bada_pp", 96), ("bgt_bc", 2 * D), ("gmix_pp", 16), ("gffn_pp", 16), ("gf_bc", D),
                  ("convw_pp", 8 * CK), ("convb_pp", 8), ("lng_pp", 8), ("lnb_pp", 8), ("sink_pp", 8),
                  ("ident", 128), ("maskL", 128), ("maskR", 128), ("gmask", 4)):
        small[nm] = din(nm, [128, w])
    y_d = nc.dram_tensor("y", [NG * T, D], F32, kind="ExternalOutput").ap()
    dbg_d = {}

    from contextlib import ExitStack
    es = ExitStack()
    with es:
        def sb(name, nbytes):
            t = es.enter_context(nc.sbuf_tensor(name, [128, nbytes // 2], BF16))
            return Region(name, t, nbytes)
        XT = sb("XT", 4 * 8192)
        HT = sb("HT", 16 * E * 2)
        R2 = sb("R2", 46080)
        ZR = sb("ZR", 16384)
        TMP = sb("TMP", 4 * 2048)
        PTR = sb("PTR", 2 * 3072 + 1024)
        ROPE = sb("ROPE", 2 * E * 4)
        BC = sb("BC", 2 * 8192)
        SL = [sb("SL%d" % i, SLOT * 2) for i in range(NSLOT)]
        CST = sb("CST", 4096)
        ps_t = es.enter_context(nc.psum_tensor("ps", [128, 4096], F32))

        sem_names = ["pe", "act", "dve", "pool"] + ["sl%d" % i for i in range(NSLOT)] + ["x%d" % i for i in range(6)] \
            + ["ropec", "ropes", "cst", "cstp", "gf", "bg"] + ["dbg%d" % i for i in range(8)] + ["o%d" % i for i in range(4)]
        sems = {n: es.enter_context(nc.semaphore(n)) for n in sem_names}

        def psb(b, n=512, dt=F32, off=0):
            ap = ps_t[:, b * 512: (b + 1) * 512]
            if dt == BF16:
                ap = ap.bitcast(BF16)
            return V(ap[:, off:off + n], [("ps", b)])

        def psmulti(b0, nb):
            return V(ps_t[:, b0 * 512:(b0 + nb) * 512], [("ps", b) for b in range(b0, b0 + nb)])

        coff = {}
        o = 0
        for nm, nby in (("ident", 256), ("ones", 256), ("maskL", 256), ("maskR", 256),
                        ("convw", 8 * CK * 4), ("convb", 32), ("lng", 32), ("lnb", 32), ("lnbh", 32), ("lngh", 32),
                        ("esink", 32), ("gmask", 16), ("flags", 2 * NG * 4), ("cT", 128), ("scT", 64),
                        ("gmix", 64), ("gffn", 64), ("bada", 384),
                        ("g1", 128), ("sh1", 128), ("g2", 128), ("sh2", 128),
                        ("ss", 64), ("rstd", 64), ("ss3", 128), ("stg", 64)):
            coff[nm] = o
            o += (nby + 3) // 4 * 4
        assert o <= 4096, o

        def cst(nm, n, dt=F32, eoff=0):
            sz = 4 if dt == F32 else 2
            return CST.view(coff[nm] + eoff * sz, n, dt)

        def dump(name, v, n, dt):
            if not dbg or st["dry"]:
                return
            if name in dbg_d:
                return
            dbg_d[name] = nc.dram_tensor("dbg_" + name, [128, n], dt, kind="ExternalOutput").ap()
            P.dma("sp", "dbg%d" % (len(dbg_d) - 1), lambda e, o=dbg_d[name], i=v.ap: e.dma_start(out=o, in_=i), reads=v.res)

        def dma_sp(sem, out_v, in_ap, reads=()):
            P.dma("sp", sem, lambda e, o=out_v.ap, i=in_ap: e.dma_start(out=o, in_=i), reads=reads, writes=out_v.res)

        def dma_cast(sem, out_v, in_ap):
            P.dma("pool", sem, lambda e, o=out_v.ap, i=in_ap: e.dma_start(out=o, in_=i), writes=out_v.res)

        def rr(*vs):
            r = []
            for v in vs:
                r += v.res
            return r

        slab_sched = []
        st = {"next_load": 0, "next_use": 0, "dry": True}

        def slab_view(i, n):
            return SL[i % NSLOT].view(0, n, BF16)

        def prefetch(upto, extra=()):
            upto = min(upto, st.get("cap", len(slab_sched)) - 1)
            while st["next_load"] <= min(upto, len(slab_sched) - 1):
                i = st["next_load"]
                nm = slab_sched[i]
                n = wd[nm].shape[1]
                sv_ = slab_view(i, n)
                P.dma("pool", "sl%d" % (i % NSLOT), lambda e, o=sv_.ap, i_=wd[nm]: e.dma_start(out=o, in_=i_),
                      reads=list(extra), writes=sv_.res)
                st["next_load"] += 1

        def use_slab(expect):
            if st["dry"]:
                slab_sched.append(expect)
                return slab_view(len(slab_sched) - 1, wd[expect].shape[1])
            i = st["next_use"]
            assert slab_sched[i] == expect, (slab_sched[i], expect)
            prefetch(i + NSLOT - 1)
            st["next_use"] += 1
            n = wd[expect].shape[1]
            return slab_view(i, n)

        def emit_all():
            for nm, src in (("convw", "convw_pp"), ("convb", "convb_pp"), ("lng", "lng_pp"), ("lnb", "lnb_pp"),
                            ("esink", "sink_pp"), ("gmask", "gmask"), ("gmix", "gmix_pp"), ("gffn", "gffn_pp"), ("bada", "bada_pp")):
                w = small[src].shape[1]
                dma_sp("cst", cst(nm, w), small[src])
            dma_sp("cst", cst("flags", 2 * NG), flags_d)
            dma_sp("cst", cst("cT", 32), cT_d)
            for nm in ("ident", "maskL", "maskR"):
                P.dma("pool", "cstp", lambda e, o=cst(nm, 128, BF16).ap, i=small[nm]: e.dma_start(out=o, in_=i),
                      writes=cst(nm, 128, BF16).res)
            for en_ in ("pe", "act", "dve"):
                P.ops[en_].append(([("cst", P.cnt["cst"]), ("cstp", P.cnt["cstp"])], None, None, 0))
                P.waited[en_]["cst"] = P.cnt["cst"]
                P.waited[en_]["cstp"] = P.cnt["cstp"]
            ones_v = cst("ones", 128, BF16)
            P.op("dve", lambda e: e.memset(ones_v.ap, 1.0), writes=ones_v.res)
            cw_v = cst("convw", 8 * CK)
            P.op("dve", lambda e: e.tensor_scalar(cw_v.ap, cw_v.ap, 0.5, None, ALU.mult), reads=cw_v.res, writes=cw_v.res)
            es_v = cst("esink", 8)
            P.op("act", lambda e: e.activation(es_v.ap, es_v.ap, AF.Exp), reads=es_v.res, writes=es_v.res)
            cT_v = cst("cT", 32)
            scT_v = cst("scT", 32, BF16)
            P.op("act", lambda e: e.activation(scT_v.ap, cT_v.ap, AF.Silu), reads=cT_v.res, writes=scT_v.res)

            screp = PTR.view(0, 16 * 128, BF16)

            def build_screp(slot):
                for k in range(16):
                    src = scT_v.ap[:, slot * 16 + k: slot * 16 + k + 1].to_broadcast([128, 128])
                    dst = screp.ap[:, k * 128:(k + 1) * 128]
                    P.op("dve", lambda e, d=dst, s=src: e.tensor_copy(out=d, in_=s), reads=scT_v.res, writes=screp.res)

            modpp = {0: "sh1", 1: "g1", 3: "sh2", 4: "g2"}
            bank_rr = [0]

            def nbank():
                b = bank_rr[0]
                bank_rr[0] = (b + 1) % 8
                return b

            def mod_pp_block(blk, s4):
                sv = use_slab("wada%d" % (blk * 4 + s4))
                w3 = sv.ap.rearrange("p (k n) -> p k n", k=16)
                b = nbank()
                pv = psb(b, 8)

                def mm(e):
                    ins = None
                    for m in range(4):
                        for k in range(16):
                            ins = e.matmul(pv.ap[:, 2 * m:2 * m + 2], w3[:, k, m * 128:(m + 1) * 128],
                                           scT_v.ap.rearrange("p (s k) -> p k s", s=2)[:, k, :],
                                           start=(k == 0), stop=(k == 15))
                    return ins
                P.op("pe", mm, reads=sv.res + scT_v.res, writes=pv.res)
                nm = modpp[blk]
                for m in range(4):
                    ch = s4 * 4 + m
                    for slot in range(2):
                        dst = cst(nm, 1, F32, slot * 16 + ch)
                        bia = cst("bada", 1, F32, blk * 16 + ch)
                        P.op("dve", lambda e, d=dst, s=pv.ap[:, 2 * m + slot:2 * m + slot + 1], bb=bia:
                             e.tensor_scalar(d.ap, s, bb.ap, None, ALU.add), reads=pv.res + bia.res, writes=dst.res)

            def mod_row_block(which, s4, slot):
                blk = 2 if which == 0 else 5
                sv = use_slab("wada%d" % (blk * 4 + s4))
                w3 = sv.ap.rearrange("p (k n) -> p k n", k=16)
                b = nbank()
                pv = psb(b)

                def mm(e):
                    ins = None
                    for k in range(16):
                        ins = e.matmul(pv.ap, screp.ap[:, k * 128:(k + 1) * 128], w3[:, k, :], start=(k == 0), stop=(k == 15))
                    return ins
                P.op("pe", mm, reads=sv.res + screp.res, writes=pv.res)
                dst = BC.view(which * 8192 + s4 * 2048, 512, F32)
                bsrc = TMP.view(0, 512, F32)
                dma_sp("bg", bsrc, small["bgt_bc"][:, which * D + s4 * 512: which * D + (s4 + 1) * 512])
                P.op("dve", lambda e: e.tensor_tensor(dst.ap, pv.ap, bsrc.ap, ALU.add), reads=pv.res + bsrc.res, writes=dst.res)
                if which == 0:
                    P.op("dve", lambda e: e.tensor_scalar(dst.ap, dst.ap, 0.5, None, ALU.mult), reads=dst.res, writes=dst.res)

            def mod_finish(nm, gn):
                for slot in range(2):
                    d = cst(nm, 16, F32, slot * 16)
                    gsrc = cst(gn, 16)
                    P.op("dve", lambda e, d=d, gsrc=gsrc: e.scalar_tensor_tensor(d.ap, d.ap, 1.0, gsrc.ap, ALU.add, ALU.mult),
                         reads=d.res + gsrc.res, writes=d.res)

            def mod_phase_first(slot):
                for blk in (0, 1):
                    for s4 in range(4):
                        mod_pp_block(blk, s4)
                mod_finish("g1", "gmix")

            def mod_rows(which, slot):
                build_screp(slot)
                for s4 in range(4):
                    mod_row_block(which, s4, slot)

            def mod_ffn_pp():
                for blk in (3, 4):
                    for s4 in range(4):
                        mod_pp_block(blk, s4)
                mod_finish("g2", "gffn")

            def xt(e):
                return XT.view(e * 8192, D, F32)

            def xh(i):
                return ZR.view(i * 8192, D, F32)

            def xtile(e6):
                return xh(0) if e6 == 0 else (xh(1) if e6 == 5 else xt(e6 - 1))

            def hT(c, lo=0, hi=E):
                return HT.view(c * E * 2 + lo * 2, hi - lo, BF16)

            def h2T(c, lo=0, hi=T):
                return HT.view(c * T * 2 + lo * 2, hi - lo, BF16)

            def xnb(e6):
                return R2.view(e6 * 4096, D, BF16)

            JUNK_OFF = 6 * 4096
            O_QT, O_KT, O_VA, O_HG, O_OT, O_CT = 0, 8192, 8192 + 3072, 8192 + 3072 + 6144, 29696, 37888

            def qT(half, j, lo=0, hi=T):
                return R2.view(O_QT + (half * 4 + j) * 1024 + lo * 2, hi - lo, BF16)

            def kT(half, lo=0, hi=E):
                return R2.view(O_KT + half * 1536 + lo * 2, hi - lo, BF16)

            def vaug(e6):
                return R2.view(O_VA + e6 * 1024, 512, BF16)

            def KZ(g4, half, lo=0, hi=E):
                return R2.view(O_HG + (g4 * 2 + half) * 1536 + lo * 2, hi - lo, BF16)

            def vden(e6):
                return ROPE.view(e6 * 1024, 512, BF16)

            def hg(c, lo=0, hi=E):
                return R2.view(O_HG + c * 1536 + lo * 2, hi - lo, BF16)

            def oT(ci, lo=0, hi=T):
                return R2.view(O_OT + ci * 1024 + lo * 2, hi - lo, BF16)

            def cTb(c, lo=0, hi=T):
                return R2.view(O_CT + c * 1024 + lo * 2, hi - lo, BF16)

            def actT(j, lo=0, hi=T):
                return R2.view(j * 1024 + lo * 2, hi - lo, BF16)

            def zt(c):
                return ZR.view(c * 2048, 512, F32)

            def mixT(c, lo=0, hi=T):
                return ZR.view(c * 1024 + lo * 2, hi - lo, BF16)

            def tmp(i, n=512, dt=F32):
                return TMP.view(i * 2048, n, dt)

            ident_v = cst("ident", 128, BF16)
            maskL_v = cst("maskL", 128, BF16)
            maskR_v = cst("maskR", 128, BF16)

            def rstd_batch(src, dst, scale):
                P.op("dve", lambda e: e.tensor_scalar(dst.ap, src.ap, scale, EPS, ALU.mult, ALU.add), reads=src.res, writes=dst.res)
                P.op("act", lambda e: e.activation(dst.ap, dst.ap, AF.Sqrt), reads=dst.res, writes=dst.res)
                P.op("dve", lambda e: e.reciprocal(dst.ap, dst.ap), reads=dst.res, writes=dst.res)

            def rmsnorm_tiles(pairs, pre=None):
                n = len(pairs)
                ssv = cst("ss", n)
                rsv = cst("rstd", n)
                if pre is not None:
                    pv_, ptags = pre
                    P.op("dve", lambda e: e.tensor_reduce(ssv.ap, pv_.ap.rearrange("p (n q) -> p n q", q=4), AX.X, ALU.add),
                         reads=pv_.res + ptags, writes=ssv.res)
                else:
                    cols = [("sscol", i) for i in range(n)]
                    P.op("dve", lambda e: e.memset(ssv.ap, 0.0), writes=ssv.res + cols)
                    for i, (xv, nbv) in enumerate(pairs):
                        acc = ssv.ap[:, i:i + 1]
                        if i % 2 == 0:
                            P.op("act", lambda e, xv=xv, nbv=nbv, acc=acc: e.activation(nbv.ap, xv.ap, AF.Square, accum_out=acc),
                                 reads=xv.res, writes=nbv.res + [cols[i]])
                        else:
                            P.op("dve", lambda e, xv=xv, nbv=nbv, acc=acc: e.scalar_tensor_tensor(nbv.ap, xv.ap, 1.0, xv.ap, ALU.mult, ALU.mult, accum_out=acc),
                                 reads=xv.res, writes=nbv.res + [cols[i]])
                    P.op("dve", lambda e: e.tensor_copy(out=ssv.ap, in_=ssv.ap), reads=cols + ssv.res, writes=ssv.res)
                rstd_batch(ssv, rsv, 1.0 / D)
                for i, (xv, nbv) in enumerate(pairs):
                    if i % 2 == 0:
                        P.op("dve", lambda e, i=i, xv=xv, nbv=nbv: e.tensor_scalar(nbv.ap, xv.ap, rsv.ap[:, i:i + 1], None, ALU.mult),
                             reads=xv.res + rsv.res, writes=nbv.res)
                    else:
                        P.op("act", lambda e, i=i, xv=xv, nbv=nbv: e.activation(nbv.ap, xv.ap, AF.Copy, scale=rsv.ap[:, i:i + 1]),
                             reads=xv.res + rsv.res, writes=nbv.res)

            def group(g):
                slot = group_seq[g]
                modg = (g == 0) or (group_seq[g] != group_seq[g - 1])
                if st["dry"]:
                    st.setdefault("gstart", {})[g] = len(slab_sched)
                else:
                    st["cap"] = st["gstart"].get(g + 1, len(slab_sched))
                r0 = group_row0[g]
                fl = [cst("flags", 1, F32, 2 * g), cst("flags", 1, F32, 2 * g + 1)]
                rc_v = ROPE.view(0, E, F32)
                rs_v = ROPE.view(E * 4, E, F32)

                def load_halo_rope(gg):
                    rr0 = group_row0[gg]
                    for e6 in (0, 5):
                        dma_sp("x%d" % e6, xtile(e6), xin[rr0 + e6 * 128: rr0 + (e6 + 1) * 128, :])
                    dma_sp("ropec", rc_v, ropec_d[gg])
                    dma_sp("ropes", rs_v, ropes_d[gg])

                if g == 0 or dbg:
                    load_halo_rope(g)
                if g == 0 or dbg:
                    for e6 in (1, 2, 3, 4):
                        dma_sp("x%d" % e6, xtile(e6), xin[r0 + e6 * 128: r0 + (e6 + 1) * 128, :])
                rmsnorm_tiles([(xtile(e6), xnb(e6)) for e6 in (0, 5, 1, 2, 3, 4)])
                for c in range(NCH):
                    b = nbank()
                    pv = psb(b, E, BF16)

                    def tr(e, c=c, pv=pv):
                        ins = None
                        for e6 in range(6):
                            ins = e.transpose(pv.ap[:, e6 * 128:(e6 + 1) * 128], xnb(e6).ap[:, c * 128:(c + 1) * 128], ident_v.ap)
                        return ins
                    P.op("pe", tr, reads=rr(*[xnb(e6) for e6 in range(6)]) + ident_v.res, writes=pv.res)
                    g1 = cst("g1", 1, F32, slot * 16 + c)
                    s1 = cst("sh1", 1, F32, slot * 16 + c)
                    hv = hT(c)
                    if c % 2 == 0:
                        P.op("act", lambda e, hv=hv, pv=pv, g1=g1, s1=s1: e.activation(hv.ap, pv.ap, AF.Identity, bias=s1.ap, scale=g1.ap),
                             reads=pv.res + g1.res + s1.res, writes=hv.res)
                    else:
                        P.op("dve", lambda e, hv=hv, pv=pv, g1=g1, s1=s1: e.tensor_scalar(hv.ap, pv.ap, g1.ap, s1.ap, ALU.mult, ALU.add),
                             reads=pv.res + g1.res + s1.res, writes=hv.res)

                def proj(sv_ap3, mcol, rhs_lo, rhs_hi, pv, extra_reads, fine=False):
                    if fine:
                        for k in range(16):
                            P.op("pe", lambda e, k=k: e.matmul(pv.ap, sv_ap3[:, k, mcol:mcol + 128], hT(k, rhs_lo, rhs_hi).ap, start=(k == 0), stop=(k == 15)),
                                 reads=extra_reads + hT(k, rhs_lo, rhs_hi).res, writes=pv.res)
                        return
                    def mm(e):
                        ins = None
                        for k in range(16):
                            ins = e.matmul(pv.ap, sv_ap3[:, k, mcol:mcol + 128], hT(k, rhs_lo, rhs_hi).ap, start=(k == 0), stop=(k == 15))
                        return ins
                    P.op("pe", mm, reads=extra_reads + rr(*[hT(k, rhs_lo, rhs_hi) for k in range(16)]), writes=pv.res)

                dump("hT", HT.view(0, 16 * E, BF16), 16 * E, BF16)
                if stop_at == 1:
                    raise _Stop()
                for half in range(2):
                    svq = use_slab("win%d" % half)
                    q3 = svq.ap.rearrange("p (k n) -> p k n", k=16)
                    for j in range(4):
                        proj(q3, j * 128, 128, 640, psb(half * 4 + j), svq.res, fine=(half == 0 and j < 2))

                def rope_pair(x1v, x2v, o1v, o2v, lo, hi):
                    n = hi - lo
                    c_ap, s_ap = rc_v.ap[:, lo:hi], rs_v.ap[:, lo:hi]
                    t1, t2 = tmp(0, n), tmp(1, n)
                    t3, t4 = tmp(2, n), tmp(3, n)
                    rd = rc_v.res + rs_v.res
                    P.op("dve", lambda e: e.tensor_tensor(t1.ap, x1v.ap, c_ap, ALU.mult), reads=x1v.res + rd, writes=t1.res)
                    P.op("dve", lambda e: e.tensor_tensor(t2.ap, x2v.ap, s_ap, ALU.mult), reads=x2v.res + rd, writes=t2.res)
                    P.op("dve", lambda e: e.tensor_tensor(o1v.ap, t1.ap, t2.ap, ALU.subtract), reads=t1.res + t2.res, writes=o1v.res)
                    P.op("dve", lambda e: e.tensor_tensor(t3.ap, x2v.ap, c_ap, ALU.mult), reads=x2v.res + rd, writes=t3.res)
                    P.op("dve", lambda e: e.tensor_tensor(t4.ap, x1v.ap, s_ap, ALU.mult), reads=x1v.res + rd, writes=t4.res)
                    P.op("dve", lambda e: e.tensor_tensor(o2v.ap, t3.ap, t4.ap, ALU.add), reads=t3.res + t4.res, writes=o2v.res)

                for j in range(4):
                    rope_pair(psb(j), psb(4 + j), qT(0, j), qT(1, j), 128, 640)
                svk = use_slab("win2")
                k3 = svk.ap.rearrange("p (k n) -> p k n", k=16)
                for half in range(2):
                    for hh in range(2):
                        proj(k3, half * 128, hh * 384, (hh + 1) * 384, psb(half * 2 + hh, 384), svk.res)
                for hh in range(2):
                    rope_pair(psb(hh, 384), psb(2 + hh, 384), kT(0, hh * 384, (hh + 1) * 384), kT(1, hh * 384, (hh + 1) * 384),
                              hh * 384, (hh + 1) * 384)
                svv = use_slab("win3")
                v3 = svv.ap.rearrange("p (k n) -> p k n", k=16)
                for e6 in range(6):
                    pv = psb(4 + (e6 % 4), 256)

                    def mm(e, e6=e6, pv=pv):
                        ins = None
                        for k in range(16):
                            ins = e.matmul(pv.ap, hT(k, e6 * 128, (e6 + 1) * 128).ap, v3[:, k, :], start=(k == 0), stop=(k == 15))
                        return ins
                    P.op("pe", mm, reads=svv.res + rr(*[hT(k, e6 * 128, (e6 + 1) * 128) for k in range(16)]), writes=pv.res)
                    va, vd = vaug(e6), vden(e6)
                    vo5 = va.ap.rearrange("p (a b c) -> p a b c", a=2, b=2)
                    vd5 = vd.ap.rearrange("p (a b c) -> p a b c", a=2, b=2)
                    pv4 = pv.ap.rearrange("p (a b d) -> p a b d", a=2, b=2)
                    P.op("dve", lambda e, va=va: e.memset(va.ap, 0.0), writes=va.res)
                    P.op("dve", lambda e, vd=vd: e.memset(vd.ap, 0.0), writes=vd.res)
                    for gl in range(2):
                        o_ = vo5[:, :, gl, gl * 64:(gl + 1) * 64]
                        d_ = vd5[:, :, gl, gl * 64:(gl + 1) * 64]
                        i_ = pv4[:, :, gl, :]
                        if e6 in (0, 5):
                            f = fl[0] if e6 == 0 else fl[1]
                            P.op("dve", lambda e, o_=o_, i_=i_, f=f: e.tensor_scalar(o_, i_, f.ap, None, ALU.mult),
                                 reads=pv.res + f.res, writes=va.res)
                            P.op("dve", lambda e, d_=d_, f=f: e.tensor_copy(out=d_, in_=f.ap.unsqueeze(1).to_broadcast([128, 2, 64])),
                                 reads=f.res, writes=vd.res)
                        else:
                            P.op("act", lambda e, o_=o_, i_=i_: e.activation(o_, i_, AF.Copy), reads=pv.res, writes=va.res)
                            P.op("dve", lambda e, d_=d_: e.memset(d_, 1.0), writes=vd.res)
                UL, UN = 113, 271
                NPE = CK

                def dve_taps(c):
                    zv = zt(c)
                    cb = cst("convb", 1, F32, c)
                    for j in range(NPE, CK):
                        wj = cst("convw", 1, F32, c * CK + j)
                        hv = hg(c, j, j + T)
                        if j == NPE:
                            P.op("dve", lambda e, zv=zv, hv=hv, wj=wj, cb=cb: e.tensor_scalar(zv.ap, hv.ap, wj.ap, cb.ap, ALU.mult, ALU.add),
                                 reads=hv.res + wj.res + cb.res, writes=zv.res)
                        else:
                            P.op("dve", lambda e, zv=zv, hv=hv, wj=wj: e.scalar_tensor_tensor(zv.ap, hv.ap, wj.ap, zv.ap, ALU.mult, ALU.add),
                                 reads=hv.res + wj.res + zv.res, writes=zv.res)

                for s in range(4):
                    svu = use_slab("win%d" % (4 + s))
                    u3 = svu.ap.rearrange("p (k n) -> p k n", k=16)
                    for cc in range(2):
                        c = 2 * s + cc
                        base = (c % 2) * 4
                        for hh in range(2):
                            lo = UL + hh * UN
                            proj(u3, cc * 256, lo, lo + UN, psb(base + hh, UN), svu.res)
                            proj(u3, cc * 256 + 128, lo, lo + UN, psb(base + 2 + hh, UN), svu.res)
                        for hh in range(2):
                            th = tmp(hh, UN)
                            pa, pg = psb(base + hh, UN), psb(base + 2 + hh, UN)
                            hv = hg(c, hh * UN, (hh + 1) * UN)
                            P.op("act", lambda e, th=th, pg=pg: e.activation(th.ap, pg.ap, AF.Tanh, scale=0.5), reads=pg.res, writes=th.res)
                            P.op("dve", lambda e, th=th, pa=pa, hv=hv: e.scalar_tensor_tensor(hv.ap, th.ap, 1.0, pa.ap, ALU.add, ALU.mult),
                                 reads=th.res + pa.res, writes=hv.res)
                        for side in range(2):
                            hv = hg(c, 0, 15) if side == 0 else hg(c, 15 + T, 30 + T)
                            P.op("dve", lambda e, hv=hv, f=fl[side]: e.tensor_scalar(hv.ap, hv.ap, f.ap, None, ALU.mult),
                                 reads=hv.res + f.res, writes=hv.res)
                        if c > 0 and NPE < CK:
                            dve_taps(c - 1)
                if NPE < CK:
                    dve_taps(7)
                if stop_at == 2:
                    raise _Stop()
                s1p, s2p = psb(0), psb(1)
                pend_stats = []

                def emit_stats(zb, zq, c):
                    P.op("pe", lambda e: e.matmul(s1p.ap, ones_v.ap, zb.ap, start=(c == 0), stop=(c == 7)),
                         reads=zb.res + ones_v.res, writes=s1p.res)
                    P.op("pe", lambda e: e.matmul(s2p.ap, ones_v.ap, zq.ap, start=(c == 0), stop=(c == 7)),
                         reads=zq.res + ones_v.res, writes=s2p.res)

                for c in range(8):
                    pz = psb(2 + (c % 6))
                    for hi_, (j0, j1) in enumerate(((0, NPE // 2), (NPE // 2, NPE))):
                        dg = TMP.view(hi_ * 4096, (j1 - j0) * 128, BF16)
                        nt = j1 - j0
                        wv = cst("convw", nt, F32, c * CK + j0)
                        dg3 = dg.ap.rearrange("p (t m) -> p t m", t=nt)
                        beng = "dve" if hi_ == 0 else "pool"
                        P.op(beng, lambda e, dg3=dg3, wv=wv, nt=nt: e.tensor_tensor(
                            dg3, ident_v.ap.unsqueeze(1).to_broadcast([128, nt, 128]),
                            wv.ap.unsqueeze(2).to_broadcast([128, nt, 128]), ALU.mult),
                             reads=ident_v.res + wv.res, writes=dg.res)

                        def cmm(en, c=c, dg=dg, j0=j0, j1=j1, pz=pz):
                            ins = None
                            for j in range(j0, j1):
                                ins = en.matmul(pz.ap, dg.ap[:, (j - j0) * 128:(j - j0 + 1) * 128], hg(c, j, j + T).ap,
                                                start=(j == 0), stop=(j == NPE - 1))
                            return ins
                        P.op("pe", cmm, reads=dg.res + hg(c, 0, 30 + T).res, writes=pz.res)
                    zv = zt(c)
                    zb = PTR.view((c % 2) * 2048, 512, BF16)
                    zq = PTR.view((c % 2) * 2048 + 1024, 512, BF16)
                    if NPE < CK:
                        P.op("dve", lambda e, zv=zv, pz=pz: e.tensor_tensor(zv.ap, pz.ap, zv.ap, ALU.add), reads=pz.res + zv.res, writes=zv.res)
                    else:
                        cb = cst("convb", 1, F32, c)
                        P.op("act", lambda e, zv=zv, pz=pz, cb=cb: e.activation(zv.ap, pz.ap, AF.Identity, bias=cb.ap), reads=pz.res + cb.res, writes=zv.res)
                    P.op("act", lambda e, zb=zb, zv=zv: e.activation(zb.ap, zv.ap, AF.Copy), reads=zv.res, writes=zb.res)
                    P.op("act", lambda e, zq=zq, zv=zv: e.activation(zq.ap, zv.ap, AF.Square), reads=zv.res, writes=zq.res)
                    pend_stats.append((zb, zq, c))
                    if len(pend_stats) > 1:
                        emit_stats(*pend_stats.pop(0))
                emit_stats(*pend_stats.pop(0))
                for g4 in range(4):
                    gmk = cst("gmask", 1, F32, g4)
                    for half in range(2):
                        kz, ks = KZ(g4, half), kT(half)
                        P.op("dve", lambda e, kz=kz, ks=ks, gmk=gmk: e.tensor_scalar(kz.ap, ks.ap, gmk.ap, None, ALU.mult),
                             reads=ks.res + gmk.res, writes=kz.res)
                mu, rsd, m2 = tmp(0), tmp(1), tmp(2)
                P.op("dve", lambda e: e.tensor_scalar(mu.ap, s1p.ap, 1.0 / CD, None, ALU.mult), reads=s1p.res, writes=mu.res)
                P.op("dve", lambda e: e.tensor_tensor(m2.ap, mu.ap, mu.ap, ALU.mult), reads=mu.res, writes=m2.res)
                P.op("dve", lambda e: e.scalar_tensor_tensor(rsd.ap, s2p.ap, 1.0 / CD, m2.ap, ALU.mult, ALU.subtract),
                     reads=s2p.res + m2.res, writes=rsd.res)
                rstd_batch(rsd, rsd, 1.0)
                def ln_apply(c):
                    zv = zt(c)
                    P.op("dve", lambda e, zv=zv: e.tensor_tensor(zv.ap, zv.ap, mu.ap, ALU.subtract), reads=zv.res + mu.res, writes=zv.res)
                    P.op("dve", lambda e, zv=zv: e.tensor_tensor(zv.ap, zv.ap, rsd.ap, ALU.mult), reads=zv.res + rsd.res, writes=zv.res)
                    lg, lb = cst("lng", 1, F32, c), cst("lnb", 1, F32, c)
                    cv = cTb(c)
                    P.op("act", lambda e, cv=cv, zv=zv, lg=lg, lb=lb: e.activation(cv.ap, zv.ap, AF.Silu, bias=lb.ap, scale=lg.ap),
                         reads=zv.res + lg.res + lb.res, writes=cv.res)
                if stop_at == 3:
                    raise _Stop()
                def att_s(stp):
                    e, j = stp // 4, stp % 4
                    Sv = psmulti((stp % 2) * 3, 3)
                    S4 = Sv.ap.rearrange("p (g k q) -> p g k q", g=4, k=3)

                    def smm(en, e=e, j=j, S4=S4):
                        ins = None
                        for kb in range(3):
                            for g4 in range(4):
                                for half in range(2):
                                    ins = en.matmul(S4[:, g4, kb, :],
                                                    KZ(g4, half, (e + kb) * 128, (e + kb + 1) * 128).ap,
                                                    qT(half, j, e * 128, (e + 1) * 128).ap,
                                                    start=(half == 0), stop=(half == 1))
                        return ins
                    P.op("pe", smm, reads=rr(*[KZ(g4, half, e * 128, (e + 3) * 128) for g4 in range(4) for half in range(2)])
                         + rr(qT(0, j, e * 128, (e + 1) * 128), qT(1, j, e * 128, (e + 1) * 128)), writes=Sv.res)

                def att_p(stp):
                    e, j = stp // 4, stp % 4
                    Sv = psmulti((stp % 2) * 3, 3)
                    Pv = PTR.view((stp % 2) * 3072, 1536, BF16)
                    P4 = Pv.ap.rearrange("p (g k q) -> p g k q", g=4, k=3)
                    for b3 in range(3):
                        P.op("act", lambda en, Pv=Pv, Sv=Sv, b3=b3: en.activation(Pv.ap[:, b3 * 512:(b3 + 1) * 512], Sv.ap[:, b3 * 512:(b3 + 1) * 512], AF.Exp, scale=0.125),
                             reads=Sv.res, writes=Pv.res)
                    P.op("dve", lambda en, P4=P4: en.tensor_tensor(P4[:, :, 0, :], P4[:, :, 0, :],
                                                                    maskL_v.ap.unsqueeze(1).to_broadcast([128, 4, 128]), ALU.mult),
                         reads=Pv.res + maskL_v.res, writes=Pv.res)
                    P.op("pool", lambda en, P4=P4: en.tensor_tensor(P4[:, :, 2, :], P4[:, :, 2, :],
                                                                   maskR_v.ap.unsqueeze(1).to_broadcast([128, 4, 128]), ALU.mult),
                         reads=Pv.res + maskR_v.res, writes=Pv.res)

                def att_o(stp):
                    e, j = stp // 4, stp % 4
                    Pv = PTR.view((stp % 2) * 3072, 1536, BF16)
                    P4 = Pv.ap.rearrange("p (g k q) -> p g k q", g=4, k=3)
                    ODv = psb(6 + (stp % 2))
                    OD3 = ODv.ap.rearrange("p (a q) -> p a q", a=4)

                    def pvm(en, e=e, P4=P4, OD3=OD3):
                        ins = None
                        for pi in range(2):
                            for part in range(2):
                                n_ = 0
                                for gl in range(2):
                                    for kb in range(3):
                                        src = vaug(e + kb) if part == 0 else vden(e + kb)
                                        l3 = src.ap.rearrange("p (g d) -> p g d", g=4)
                                        ins = en.matmul(OD3[:, part * 2 + pi, :], l3[:, 2 * pi + gl, :], P4[:, 2 * pi + gl, kb, :],
                                                        start=(n_ == 0), stop=(n_ == 5))
                                        n_ += 1
                        return ins
                    P.op("pe", pvm, reads=Pv.res + rr(*[vaug(e + kb) for kb in range(3)]) + rr(*[vden(e + kb) for kb in range(3)]),
                         writes=ODv.res)

                def att_n(stp):
                    e, j = stp // 4, stp % 4
                    ODv = psb(6 + (stp % 2))
                    OD3 = ODv.ap.rearrange("p (a q) -> p a q", a=4)
                    Dp = PTR.view(6144, 256, F32)
                    Dp3 = Dp.ap.rearrange("p (a q) -> p a q", a=2)
                    esk = cst("esink", 1, F32, j)
                    for pi in range(2):
                        esk = cst("esink", 1, F32, pi * 4 + j)
                        P.op("dve", lambda en, pi=pi, esk=esk, OD3=OD3, Dp3=Dp3: en.tensor_scalar(Dp3[:, pi, :], OD3[:, 2 + pi, :], esk.ap, None, ALU.add),
                             reads=ODv.res + esk.res, writes=Dp.res)
                    P.op("dve", lambda en, Dp=Dp: en.reciprocal(Dp.ap, Dp.ap), reads=Dp.res, writes=Dp.res)
                    for pi in range(2):
                        ov = oT(pi * 4 + j, e * 128, (e + 1) * 128)
                        P.op("dve", lambda en, pi=pi, ov=ov, OD3=OD3, Dp3=Dp3: en.tensor_tensor(ov.ap, OD3[:, pi, :], Dp3[:, pi, :], ALU.mult),
                             reads=ODv.res + Dp.res, writes=ov.res)

                att_s(0)
                att_p(0)
                for stp in range(16):
                    if stp + 1 < 16:
                        att_s(stp + 1)
                    att_o(stp)
                    if stp + 1 < 16:
                        att_p(stp + 1)
                    att_n(stp)
                for c in range(8):
                    ln_apply(c)
                dump("qT", R2.view(O_QT, 8 * T, BF16), 8 * T, BF16)
                dump("kT", R2.view(O_KT, 2 * E, BF16), 2 * E, BF16)
                dump("va", R2.view(O_VA, 6 * 512, BF16), 6 * 512, BF16)
                dump("oT", R2.view(O_OT, 8 * T, BF16), 8 * T, BF16)
                dump("cT", R2.view(O_CT, 8 * T, BF16), 8 * T, BF16)
                if stop_at == 4:
                    raise _Stop()
                for c in range(NCH):
                    svg = use_slab("wg%d" % c)
                    a3 = svg.ap[:, 0:1024].rearrange("p (k n) -> p k n", k=8)
                    c3 = svg.ap[:, 1024:2048].rearrange("p (k n) -> p k n", k=8)
                    ga3 = svg.ap[:, 2048:4096].rearrange("p (k n) -> p k n", k=16)
                    gc3 = svg.ap[:, 4096:6144].rearrange("p (k n) -> p k n", k=16)
                    cc = 0
                    base = (c % 2) * 4
                    pga, pgc, pao, pco = psb(base), psb(base + 1), psb(base + 2), psb(base + 3)
                    proj(ga3, 0, 128, 640, pga, svg.res)
                    proj(gc3, 0, 128, 640, pgc, svg.res)

                    def aom(en, cc=cc, a3=a3, pao=pao):
                        ins = None
                        for k in range(8):
                            ins = en.matmul(pao.ap, a3[:, k, cc * 128:(cc + 1) * 128], oT(k).ap, start=(k == 0), stop=(k == 7))
                        return ins
                    P.op("pe", aom, reads=svg.res + rr(*[oT(k) for k in range(8)]), writes=pao.res)

                    def com(en, cc=cc, c3=c3, pco=pco):
                        ins = None
                        for k in range(8):
                            ins = en.matmul(pco.ap, c3[:, k, cc * 128:(cc + 1) * 128], cTb(k).ap, start=(k == 0), stop=(k == 7))
                        return ins
                    P.op("pe", com, reads=svg.res + rr(*[cTb(k) for k in range(8)]), writes=pco.res)
                    tha, thc = tmp(0), tmp(1)
                    m1, m2_ = tmp(2), tmp(3)
                    P.op("act", lambda en, tha=tha, pga=pga: en.activation(tha.ap, pga.ap, AF.Tanh, scale=0.5), reads=pga.res, writes=tha.res)
                    P.op("act", lambda en, thc=thc, pgc=pgc: en.activation(thc.ap, pgc.ap, AF.Tanh, scale=0.5), reads=pgc.res, writes=thc.res)
                    P.op("dve", lambda en, m1=m1, tha=tha, pao=pao: en.scalar_tensor_tensor(m1.ap, tha.ap, 1.0, pao.ap, ALU.add, ALU.mult),
                         reads=tha.res + pao.res, writes=m1.res)
                    P.op("dve", lambda en, m2_=m2_, thc=thc, pco=pco: en.scalar_tensor_tensor(m2_.ap, thc.ap, 1.0, pco.ap, ALU.add, ALU.mult),
                         reads=thc.res + pco.res, writes=m2_.res)
                    mv = mixT(c)
                    P.op("dve", lambda en, mv=mv, m1=m1, m2_=m2_: en.tensor_tensor(mv.ap, m1.ap, m2_.ap, ALU.add),
                         reads=m1.res + m2_.res, writes=mv.res)
                if stop_at == 5:
                    raise _Stop()
                if modg:
                    mod_rows(0, slot)
                ss1 = cst("ss3", 16)
                ss1tags = [("ss1col", i) for i in range(16)]
                P.op("dve", lambda en: en.memset(ss1.ap, 0.0), writes=ss1.res + ss1tags)
                for nb in range(4):
                    svo = use_slab("wo%d" % nb)
                    o3 = svo.ap.rearrange("p (k n) -> p k n", k=16)
                    gt = BC.view(nb * 2048, 512, F32)
                    for e in range(4):
                        pv = psb(nbank())

                        def mm(en, e=e, pv=pv, o3=o3):
                            ins = None
                            for k in range(16):
                                ins = en.matmul(pv.ap, mixT(k, e * 128, (e + 1) * 128).ap, o3[:, k, :], start=(k == 0), stop=(k == 15))
                            return ins
                        P.op("pe", mm, reads=svo.res + rr(*[mixT(k, e * 128, (e + 1) * 128) for k in range(16)]), writes=pv.res)
                        tv = tmp((nb * 4 + e) % 4)
                        xv = XT.view(e * 8192 + nb * 2048, 512, F32)
                        P.op("dve", lambda en, tv=tv, pv=pv, gt=gt: en.tensor_tensor(tv.ap, pv.ap, gt.ap, ALU.mult), reads=pv.res + gt.res, writes=tv.res)
                        P.op("dve", lambda en, tv=tv, xv=xv: en.tensor_tensor(xv.ap, xv.ap, tv.ap, ALU.add), reads=tv.res + xv.res, writes=xv.res)
                        junk1 = TMP.view(((nb * 4 + e) % 4) * 2048, 512, BF16)
                        P.op("act", lambda en, xv=xv, e=e, nb=nb, junk1=junk1: en.activation(junk1.ap, xv.ap, AF.Square, accum_out=ss1.ap[:, e * 4 + nb:e * 4 + nb + 1]),
                             reads=xv.res, writes=junk1.res + [ss1tags[e * 4 + nb]])
                dump("mixT", ZR.view(0, 16 * T, BF16), 16 * T, BF16)
                dump("x1", XT.view(0, 4 * D, F32), 4 * D, F32)
                if g + 1 < NG and not dbg:
                    load_halo_rope(g + 1)
                if stop_at == 6:
                    raise _Stop()
                if g == 0:
                    mod_ffn_pp()
                rmsnorm_tiles([(xt(e), xnb(e)) for e in range(4)], pre=(ss1, ss1tags))
                for c in range(NCH):
                    pv = psb(nbank(), T, BF16)

                    def tr(en, c=c, pv=pv):
                        ins = None
                        for e in range(4):
                            ins = en.transpose(pv.ap[:, e * 128:(e + 1) * 128], xnb(e).ap[:, c * 128:(c + 1) * 128], ident_v.ap)
                        return ins
                    P.op("pe", tr, reads=rr(*[xnb(e) for e in range(4)]) + ident_v.res, writes=pv.res)
                    g2 = cst("g2", 1, F32, slot * 16 + c)
                    s2 = cst("sh2", 1, F32, slot * 16 + c)
                    hv = h2T(c)
                    if c % 2 == 0:
                        P.op("act", lambda en, hv=hv, pv=pv, g2=g2, s2=s2: en.activation(hv.ap, pv.ap, AF.Identity, bias=s2.ap, scale=g2.ap),
                             reads=pv.res + g2.res + s2.res, writes=hv.res)
                    else:
                        P.op("dve", lambda en, hv=hv, pv=pv, g2=g2, s2=s2: en.tensor_scalar(hv.ap, pv.ap, g2.ap, s2.ap, ALU.mult, ALU.add),
                             reads=pv.res + g2.res + s2.res, writes=hv.res)
                if stop_at == 7:
                    raise _Stop()
                for s in range(22):
                    svf = use_slab("wfi%d" % s)
                    f3 = svf.ap.rearrange("p (k n) -> p k n", k=16)
                    for jj in range(2):
                        j = 2 * s + jj
                        base = (j % 4) * 2
                        pa, pb = psb(base), psb(base + 1)
                        for (pv, mcol) in ((pa, jj * 256), (pb, jj * 256 + 128)):
                            if j == 0:
                                for k in range(16):
                                    P.op("pe", lambda en, pv=pv, mcol=mcol, f3=f3, k=k: en.matmul(pv.ap, f3[:, k, mcol:mcol + 128], h2T(k).ap, start=(k == 0), stop=(k == 15)),
                                         reads=svf.res + h2T(k).res, writes=pv.res)
                                continue

                            def mm(en, pv=pv, mcol=mcol, f3=f3):
                                ins = None
                                for k in range(16):
                                    ins = en.matmul(pv.ap, f3[:, k, mcol:mcol + 128], h2T(k).ap, start=(k == 0), stop=(k == 15))
                                return ins
                            P.op("pe", mm, reads=svf.res + rr(*[h2T(k) for k in range(16)]), writes=pv.res)
                        sa = tmp(j % 4)
                        av = actT(j)
                        P.op("act", lambda en, sa=sa, pa=pa: en.activation(sa.ap, pa.ap, AF.Silu), reads=pa.res, writes=sa.res)
                        P.op("dve", lambda en, av=av, sa=sa, pb=pb: en.tensor_tensor(av.ap, sa.ap, pb.ap, ALU.mult), reads=sa.res + pb.res, writes=av.res)
                gfv = HT.view(0, D, F32)
                dma_sp("gf", gfv, small["gf_bc"])
                if stop_at == 8:
                    raise _Stop()
                if modg:
                    mod_rows(1, slot)
                ss3 = [cst("stg", 4, F32, e * 4) for e in range(4)]
                ss3all = cst("stg", 16)
                P.op("dve", lambda en: en.memset(ss3all.ap, 0.0), writes=ss3all.res)
                for nb in range(4):
                    accs = [psb((nb % 2) * 4 + e) for e in range(4)]
                    for kg in range(4):
                        svo = use_slab("wfo%d" % (nb * 4 + kg))
                        o3 = svo.ap.rearrange("p (k n) -> p k n", k=11)
                        for e in range(4):
                            def mm(en, e=e, kg=kg, o3=o3, pv=accs[e]):
                                ins = None
                                for k in range(11):
                                    ins = en.matmul(pv.ap, actT(kg * 11 + k, e * 128, (e + 1) * 128).ap, o3[:, k, :],
                                                    start=(kg == 0 and k == 0), stop=(kg == 3 and k == 10))
                                return ins
                            P.op("pe", mm, reads=svo.res + rr(*[actT(kg * 11 + k, e * 128, (e + 1) * 128) for k in range(11)]), writes=accs[e].res)
                    gt = BC.view(8192 + nb * 2048, 512, F32)
                    for e in range(4):
                        pv = accs[e]
                        tv = tmp(e)
                        xv = XT.view(e * 8192 + nb * 2048, 512, F32)
                        junk = R2.view(45056, 512, BF16)
                        P.op("dve", lambda en, tv=tv, pv=pv, gt=gt: en.tensor_tensor(tv.ap, pv.ap, gt.ap, ALU.mult), reads=pv.res + gt.res, writes=tv.res)
                        P.op("dve", lambda en, tv=tv, xv=xv: en.tensor_tensor(xv.ap, xv.ap, tv.ap, ALU.add), reads=tv.res + xv.res, writes=xv.res)
                        P.op("act", lambda en, xv=xv, e=e, nb=nb, junk=junk: en.activation(junk.ap, xv.ap, AF.Square, accum_out=ss3[e].ap[:, nb:nb + 1]),
                             reads=xv.res, writes=junk.res + ss3[e].res)
                ssv4, rsv4 = cst("ss", 4), cst("rstd", 4)
                P.op("dve", lambda en: en.tensor_reduce(ssv4.ap, ss3all.ap.rearrange("p (n q) -> p n q", q=4), AX.X, ALU.add),
                     reads=ss3all.res, writes=ssv4.res)
                rstd_batch(ssv4, rsv4, 1.0 / D)
                for e in range(4):
                    rsv = cst("rstd", 1, F32, e)
                    xv = xt(e)
                    yv = HT.view(8192 + (e % 2) * 8192, D, F32)
                    P.op("dve",
                         lambda en, xv=xv, yv=yv, rsv=rsv: en.scalar_tensor_tensor(yv.ap, xv.ap, rsv.ap, gfv.ap, ALU.mult, ALU.mult),
                         reads=xv.res + rsv.res + gfv.res, writes=yv.res)
                    P.dma("sp", "o%d" % e, lambda en, yv=yv, e=e: en.dma_start(out=y_d[g * T + e * 128: g * T + (e + 1) * 128, :], in_=yv.ap),
                          reads=yv.res)
                    if g + 1 < NG and not dbg:
                        rn = group_row0[g + 1]
                        dma_sp("x%d" % (e + 1), xt(e), xin[rn + (e + 1) * 128: rn + (e + 2) * 128, :])
                if not st["dry"] and g + 1 < NG:
                    st["cap"] = st["gstart"].get(g + 2, len(slab_sched))
                    prefetch(st["gstart"][g + 1] + NSLOT - 2, extra=rr(xt(0), xt(1), xt(2), xt(3)))

            try:
                mod_phase_first(group_seq[0])
                if stop_at == 0:
                    raise _Stop()
                for g in range(NG):
                    group(g)
                assert st["dry"] or st["next_use"] == len(slab_sched), (st["next_use"], len(slab_sched))
            except _Stop:
                pass

        emit_all()
        P = Prog()
        st.update(next_load=0, next_use=0, dry=False)
        emit_all()

        fin = []
        for i in range(8):
            if P.cnt.get("dbg%d" % i):
                fin.append(("dbg%d" % i, P.cnt["dbg%d" % i]))
        for e in range(4):
            fin.append(("o%d" % e, P.cnt.get("o%d" % e, 0)))

        def replay(engname, handle):
            for (waits, fn, semk, inc) in P.ops[engname]:
                for (k, v) in waits:
                    handle.wait_ge(sems[k], v)
                if fn is None:
                    continue
                ins = fn(handle)
                ins.then_inc(sems[semk], inc)

        with nc.Block() as block:
            @block.tensor
            def _(t):
                replay("pe", t)

            @block.scalar
            def _(a):
                replay("act", a)

            @block.vector
            def _(v):
                replay("dve", v)

            @block.gpsimd
            def _(gp):
                replay("pool", gp)

            @block.sync
            def _(s):
                replay("sp", s)
                for (k, v) in fin:
                    s.wait_ge(sems[k], v)
    return nc


_SHARED_KEYS = None


def kernel(x_prompt, x_sample, c_prompt, c_sample, w_ada, b_ada, g_mix, w_in, attn_sink,
           w_attn_o, conv_w, conv_b, conv_ln_g, conv_ln_b, w_conv_o, w_out, g_ffn,
           w_ffn_in, w_ffn_out, g_final):
    inp = dict(w_ada=w_ada, b_ada=b_ada, g_mix=g_mix, w_in=w_in, attn_sink=attn_sink, w_attn_o=w_attn_o,
               conv_w=conv_w, conv_b=conv_b, conv_ln_g=conv_ln_g, conv_ln_b=conv_ln_b, w_conv_o=w_conv_o,
               w_out=w_out, g_ffn=g_ffn, w_ffn_in=w_ffn_in, w_ffn_out=w_ffn_out, g_final=g_final)
    shared = prep_shared(inp)
    xp = np.asarray(x_prompt, np.float32)
    xs = np.asarray(x_sample, np.float32)
    cp = np.asarray(c_prompt, np.float32)
    csm = np.asarray(c_sample, np.float32)
    n = N_CORES
    S_s = xs.shape[1]
    P_len = xp.shape[1] // n
    group_seq = [0] * (S_s // T) + [1] * (P_len // T)
    group_row0 = [g * T for g in range(S_s // T)] + [S_s + 256 + g * T for g in range(P_len // T)]
    nc = build_program(group_seq, group_row0)
    in_maps = []
    for c in range(n):
        m = dict(shared)
        m.update(prep_core([(xs[c], 0, S_s), (xp[0], c * P_len, P_len)], np.stack([csm[c], cp[0]], 0)))
        in_maps.append(m)
    res = run_bass_kernel_spmd(nc, in_maps, core_ids=list(range(n)))
    y_s = np.stack([res.results[c]["y"][:S_s] for c in range(n)], 0)
    y_p = np.concatenate([res.results[c]["y"][S_s:] for c in range(n)], 0)[None]
    return (y_p.astype(np.float32), y_s.astype(np.float32))
```
